# Optimizing a Trainium2 kernel written in Bass

```python
import math
import jax, jax.numpy as jnp
from jax import lax
import numpy as np

D_MODEL = 1024
BATCH = 8
SEQ = 4096
DEPTH = 1

HEAD_DIM = D_MODEL // 16
DIL_GROUPS = ((128, 1), (512, 4), (2048, 16))
N_DIL_GROUPS = len(DIL_GROUPS)
HEADS_PER_GROUP = 4
N_ATTN_HEADS = N_DIL_GROUPS * HEADS_PER_GROUP
ATTN_W = N_ATTN_HEADS * HEAD_DIM
A_OUT = HEADS_PER_GROUP * HEAD_DIM
WINDOW_STEPS = 128
Q_BLOCK = 128
SG_CHUNK = 128
SG_GROUPS = 8
SG_W = SG_GROUPS * HEAD_DIM
MEM_LEN = 256
MEM_HEADS = 4
MEM_W = MEM_HEADS * HEAD_DIM
N_BRANCH = 3
IN_W = 3 * ATTN_W + 2 * SG_W + MEM_W
SPLITS = (ATTN_W, 2 * ATTN_W, 3 * ATTN_W, 3 * ATTN_W + 2 * SG_W)
D_FF = 11 * D_MODEL // 4
CONV_W = 3
N_BUCKETS = 32
MAX_DISTANCE = 2048
LN_EPS = 1e-5
ALPHA = (2 * DEPTH) ** 0.25
BETA = (8 * DEPTH) ** -0.25
ATTN_SCALE = HEAD_DIM ** -0.5
NEG = -1e30

kernel_name = 'gated_hybrid_dilated_gmlp_memxattn_convffn'


def layer_norm(x, g, b):
    xf = x.astype(jnp.float32)
    mu = xf.mean(-1, keepdims=True)
    var = jnp.square(xf - mu).mean(-1, keepdims=True)
    return ((xf - mu) * lax.rsqrt(var + LN_EPS) * g.astype(jnp.float32) + b.astype(jnp.float32)).astype(x.dtype)


def t5_bucket(dist):
    max_exact = N_BUCKETS // 2
    n = jnp.maximum(dist, 1).astype(jnp.float32)
    large = max_exact + (jnp.log(n / max_exact) / math.log(MAX_DISTANCE / max_exact)
                         * (N_BUCKETS - max_exact)).astype(jnp.int32)
    large = jnp.minimum(large, N_BUCKETS - 1)
    return jnp.where(dist < max_exact, dist, large)


def dilated_window_attention(q, k, v, bias_tab, dilation):
    b_, s_, h_, e_ = q.shape
    L = s_ // dilation
    nb = -(-L // Q_BLOCK)
    pad = nb * Q_BLOCK - L

    def split(t):
        t = t.reshape(b_, L, dilation, h_, e_).transpose(0, 2, 3, 1, 4)
        t = jnp.pad(t, ((0, 0), (0, 0), (0, 0), (0, pad), (0, 0)))
        return t.reshape(b_, dilation, h_, nb, Q_BLOCK, e_).astype(jnp.float32)

    def with_prev(t):
        prev = jnp.pad(t, ((0, 0), (0, 0), (0, 0), (1, 0), (0, 0), (0, 0)))[:, :, :, :-1]
        return jnp.concatenate([prev, t], axis=4)

    qb = split(q)
    kb = with_prev(split(k))
    vb = with_prev(split(v))

    qi = jnp.arange(Q_BLOCK)[:, None]
    kj = jnp.arange(2 * Q_BLOCK)[None, :]
    steps = qi + Q_BLOCK - kj
    band = (steps >= 0) & (steps <= WINDOW_STEPS)
    bucket = t5_bucket(jnp.maximum(steps, 0) * dilation)
    bias = bias_tab[bucket].astype(jnp.float32).transpose(2, 0, 1)
    blk = jnp.arange(nb)[:, None, None]
    valid = band[None] & ((blk > 0) | (kj[None] >= Q_BLOCK))

    s = jnp.einsum('bdhnqe,bdhnke->bdhnqk', qb, kb) * ATTN_SCALE + bias[None, None, :, None]
    s = jnp.where(valid[None, None, None], s, NEG)
    m = s.max(-1, keepdims=True)
    p = jnp.exp(s - m)
    den = p.sum(-1, keepdims=True)
    o = jnp.einsum('bdhnqk,bdhnke->bdhnqe', p, vb) / den
    lse = (m + jnp.log(den))[..., 0]

    o = o.reshape(b_, dilation, h_, nb * Q_BLOCK, e_)[:, :, :, :L]
    o = o.transpose(0, 3, 1, 2, 4).reshape(b_, s_, h_, e_)
    lse = lse.reshape(b_, dilation, h_, nb * Q_BLOCK)[..., :L]
    lse = lse.transpose(0, 3, 1, 2).reshape(b_, s_, h_)
    return o, lse


def token_mixing_sublayer(x, mem, rel_bias, w_in, w_mem_kv, sg_ln_g, sg_ln_b, w_spatial, b_spatial,
                          w_proj_a, w_proj_b, w_proj_c, w_gate, b_gate, w_out, ln_g, ln_b):
    b_, s_, _ = x.shape
    proj = x @ w_in
    qa, ka, va, zb, qc = jnp.split(proj, SPLITS, axis=-1)

    hs = (b_, s_, N_DIL_GROUPS, HEADS_PER_GROUP, HEAD_DIM)
    qa, ka, va = qa.reshape(hs), ka.reshape(hs), va.reshape(hs)
    outs, lses = [], []
    for g, (_, dil) in enumerate(DIL_GROUPS):
        o_g, lse_g = dilated_window_attention(
            qa[:, :, g], ka[:, :, g], va[:, :, g],
            rel_bias[:, g * HEADS_PER_GROUP:(g + 1) * HEADS_PER_GROUP], dil)
        outs.append(o_g)
        lses.append(lse_g)
    wts = jax.nn.softmax(jnp.stack(lses, 0), axis=0)
    oa = (wts[..., None] * jnp.stack(outs, 0)).sum(0)
    oa = oa.reshape(b_, s_, A_OUT).astype(x.dtype)

    zb = jax.nn.gelu(zb)
    u, vg = jnp.split(zb, 2, axis=-1)
    vg = layer_norm(vg, sg_ln_g, sg_ln_b).astype(jnp.float32)
    vg = vg.reshape(b_, s_ // SG_CHUNK, SG_CHUNK, SG_GROUPS, HEAD_DIM)
    tril = jnp.tril(jnp.ones((SG_CHUNK, SG_CHUNK), jnp.float32))
    sv = jnp.einsum('gpq,bcqge->bcpge', w_spatial.astype(jnp.float32) * tril, vg)
    sv = sv + b_spatial.astype(jnp.float32).T[None, None, :, :, None]
    ob = (u.astype(jnp.float32) * sv.reshape(b_, s_, SG_W)).astype(x.dtype)

    kv = (mem @ w_mem_kv).reshape(b_, MEM_LEN, 2, MEM_HEADS, HEAD_DIM)
    kc, vc = kv[:, :, 0].astype(jnp.float32), kv[:, :, 1].astype(jnp.float32)
    qc = qc.reshape(b_, s_, MEM_HEADS, HEAD_DIM).astype(jnp.float32)
    pc = jax.nn.softmax(jnp.einsum('bshe,bmhe->bhsm', qc, kc) * ATTN_SCALE, axis=-1)
    oc = jnp.einsum('bhsm,bmhe->bshe', pc, vc).reshape(b_, s_, MEM_W).astype(x.dtype)

    gates = jax.nn.sigmoid((x @ w_gate + b_gate).astype(jnp.float32)).reshape(b_, s_, N_BRANCH, D_MODEL)
    merged = (gates[:, :, 0] * (oa @ w_proj_a).astype(jnp.float32)
              + gates[:, :, 1] * (ob @ w_proj_b).astype(jnp.float32)
              + gates[:, :, 2] * (oc @ w_proj_c).astype(jnp.float32))
    y = merged.astype(x.dtype) @ w_out
    return layer_norm(ALPHA * x + y, ln_g, ln_b)


def conv_ffn_sublayer(x, w_ffn_up, conv_w, conv_b, w_ffn_down, ln_g, ln_b):
    a, bval = jnp.split(x @ w_ffn_up, 2, axis=-1)
    a = lax.conv_general_dilated(a, conv_w[:, None, :], window_strides=(1,),
                                 padding=((CONV_W - 1, 0),),
                                 dimension_numbers=('NWC', 'WIO', 'NWC'),
                                 feature_group_count=D_FF) + conv_b
    y = (jax.nn.gelu(a) * bval) @ w_ffn_down
    return layer_norm(ALPHA * x + y, ln_g, ln_b)


def setup_inputs(seed: int = 0) -> dict:
    key = jax.random.key(seed)
    ks = jax.random.split(key, 24)
    f32 = jnp.float32

    def nrm(k, shape, scale):
        return jax.random.normal(k, shape, f32) * scale

    L = DEPTH
    in_scale = jnp.ones((IN_W,), f32).at[2 * ATTN_W:3 * ATTN_W].set(BETA)
    mem_scale = jnp.ones((2 * MEM_W,), f32).at[MEM_W:].set(BETA)
    return {
        'x': nrm(ks[0], (BATCH, SEQ, D_MODEL), 1.0),
        'mem': nrm(ks[1], (BATCH, MEM_LEN, D_MODEL), 1.0),
        'rel_bias': nrm(ks[2], (N_BUCKETS, N_ATTN_HEADS), 0.5),
        'w_in': nrm(ks[3], (L, D_MODEL, IN_W), D_MODEL ** -0.5) * in_scale,
        'w_mem_kv': nrm(ks[4], (L, D_MODEL, 2 * MEM_W), D_MODEL ** -0.5) * mem_scale,
        'sg_ln_g': 1.0 + nrm(ks[5], (L, SG_W), 0.02),
        'sg_ln_b': nrm(ks[6], (L, SG_W), 0.02),
        'w_spatial': nrm(ks[7], (L, SG_GROUPS, SG_CHUNK, SG_CHUNK), SG_CHUNK ** -0.5),
        'b_spatial': 1.0 + nrm(ks[8], (L, SG_GROUPS, SG_CHUNK), 0.1),
        'w_proj_a': nrm(ks[9], (L, A_OUT, D_MODEL), A_OUT ** -0.5 * BETA),
        'w_proj_b': nrm(ks[10], (L, SG_W, D_MODEL), SG_W ** -0.5 * BETA),
        'w_proj_c': nrm(ks[11], (L, MEM_W, D_MODEL), MEM_W ** -0.5 * BETA),
        'w_gate': nrm(ks[12], (L, D_MODEL, N_BRANCH * D_MODEL), D_MODEL ** -0.5),
        'b_gate': nrm(ks[13], (L, N_BRANCH * D_MODEL), 0.01),
        'w_out': nrm(ks[14], (L, D_MODEL, D_MODEL), D_MODEL ** -0.5 * BETA),
        'ln1_g': 1.0 + nrm(ks[15], (L, D_MODEL), 0.02),
        'ln1_b': nrm(ks[16], (L, D_MODEL), 0.02),
        'w_ffn_up': nrm(ks[17], (L, D_MODEL, 2 * D_FF), D_MODEL ** -0.5),
        'conv_w': nrm(ks[18], (L, CONV_W, D_FF), CONV_W ** -0.5),
        'conv_b': nrm(ks[19], (L, D_FF), 0.01),
        'w_ffn_down': nrm(ks[20], (L, D_FF, D_MODEL), D_FF ** -0.5 * BETA),
        'ln2_g': 1.0 + nrm(ks[21], (L, D_MODEL), 0.02),
        'ln2_b': nrm(ks[22], (L, D_MODEL), 0.02),
    }


def reference(x, mem, rel_bias, w_in, w_mem_kv, sg_ln_g, sg_ln_b, w_spatial, b_spatial,
              w_proj_a, w_proj_b, w_proj_c, w_gate, b_gate, w_out, ln1_g, ln1_b,
              w_ffn_up, conv_w, conv_b, w_ffn_down, ln2_g, ln2_b):
    h = x
    for l in range(DEPTH):
        h = token_mixing_sublayer(h, mem, rel_bias, w_in[l], w_mem_kv[l], sg_ln_g[l], sg_ln_b[l],
                                  w_spatial[l], b_spatial[l], w_proj_a[l], w_proj_b[l], w_proj_c[l],
                                  w_gate[l], b_gate[l], w_out[l], ln1_g[l], ln1_b[l])
        h = conv_ffn_sublayer(h, w_ffn_up[l], conv_w[l], conv_b[l], w_ffn_down[l], ln2_g[l], ln2_b[l])
    return h
```

```python
import math
from contextlib import ExitStack

import numpy as np
import concourse.bass as bass
import concourse.mybir as mybir
from concourse.bass_utils import run_bass_kernel_spmd

F32 = mybir.dt.float32
BF16 = mybir.dt.bfloat16
AF = mybir.ActivationFunctionType
ALU = mybir.AluOpType

SEQ = 4096
DM = 1024
KC = 8
TT = 512
NT = SEQ // TT
DIL = (1, 4, 16)
D_FF = 2816
NJ = D_FF // 128
ALPHA = float(2.0 ** 0.25)
LN_EPS = 1e-5
N_CORES = 8


class Res:
    __slots__ = ("name", "last_w", "readers")

    def __init__(self, name=""):
        self.name = name
        self.last_w = None
        self.readers = []


class Sched:
    def __init__(self, nc, stack):
        self.nc = nc
        self.stack = stack
        self.engs = ("pe", "act", "dve", "pool", "sp")
        self.sem = {}
        self.cnt = {}
        self.semobj = {}
        for e in self.engs:
            self.sem[e] = stack.enter_context(nc.semaphore("c_" + e))
            self.cnt[e] = 0
            self.semobj[id(self.sem[e])] = self.sem[e]
        self.seen = {e: {} for e in self.engs}
        self.dma_sems = {}
        self.dma_cnt = {}
        self.prog = {e: [] for e in self.engs}

    def dma_sem(self, key):
        if key not in self.dma_sems:
            s = self.stack.enter_context(self.nc.semaphore("d_" + str(key)))
            self.dma_sems[key] = s
            self.dma_cnt[key] = 0
            self.semobj[id(s)] = s
        return self.dma_sems[key]

    def _waits(self, e, reads, writes):
        need = {}

        def add(tok):
            if tok is None:
                return
            sid, v, owner = tok
            if owner == "pe" and e == "pe":
                return
            if need.get(sid, 0) < v:
                need[sid] = v

        for r in reads:
            add(r.last_w)
        for w in writes:
            add(w.last_w)
            for t in w.readers:
                add(t)
        out = []
        for sid, v in need.items():
            if self.seen[e].get(sid, 0) < v:
                self.seen[e][sid] = v
                out.append((self.semobj[sid], v))
        return out

    def op(self, e, emit, reads=(), writes=(), dma=None, attach=True):
        waits = self._waits(e, reads, writes)
        if dma is not None:
            s = self.dma_sem(dma)
            self.dma_cnt[dma] += 16
            inc = (s, 16)
            tok = (id(s), self.dma_cnt[dma], "dma")
        else:
            self.cnt[e] += 1
            inc = (self.sem[e], 1)
            tok = (id(self.sem[e]), self.cnt[e], e)
        can_attach = attach and e in ("act", "dve", "pool") and dma is None
        self.prog[e].append((waits, emit, inc, can_attach))
        for r in reads:
            r.readers.append(tok)
        for w in writes:
            w.last_w = tok
            w.readers = []
        return tok

    def wait_all(self, e, resources):
        waits = self._waits(e, resources, ())
        self.prog[e].append((waits, None, None, False))

    def barrier(self):
        allsems = [(self.sem[e], self.cnt[e]) for e in self.engs]
        allsems += [(self.dma_sems[k], self.dma_cnt[k]) for k in self.dma_sems]
        for e in self.engs:
            waits = []
            for s, v in allsems:
                if v > 0 and self.seen[e].get(id(s), 0) < v:
                    self.seen[e][id(s)] = v
                    waits.append((s, v))
            self.prog[e].append((waits, None, None, False))

    def _replay(self, e, eng):
        for waits, emit, inc, can_attach in self.prog[e]:
            att = None
            if can_attach and waits:
                att = waits[-1]
                waits = waits[:-1]
            for s, v in waits:
                eng.wait_ge(s, v)
            if emit is None:
                continue
            ins = emit(eng)
            if isinstance(ins, (list, tuple)):
                first, last = ins[0], ins[-1]
            else:
                first = last = ins
            if att is not None:
                first._wait_ge(att[0], att[1])
            last.then_inc(inc[0], inc[1])

    def emit_all(self, block):
        S = self

        @block.tensor
        def _(eng):
            S._replay("pe", eng)

        @block.scalar
        def _(eng):
            S._replay("act", eng)

        @block.vector
        def _(eng):
            S._replay("dve", eng)

        @block.gpsimd
        def _(eng):
            S._replay("pool", eng)

        @block.sync
        def _(eng):
            S._replay("sp", eng)


class Arena:
    def __init__(self, nc, stack, nunits):
        self.h = stack.enter_context(nc.sbuf_tensor("arena", [128, nunits], BF16))
        self.n = nunits
        self.top = 0
        self.peak = 0

    def alloc(self, free, dt):
        n = 1
        for f in free:
            n *= int(f)
        units = n * (2 if dt == F32 else 1)
        off = self.top
        self.last_off = off
        self.top += (units + 15) // 16 * 16
        self.peak = max(self.peak, self.top)
        assert self.top <= self.n, ("SBUF arena overflow", self.top, self.n)
        v = self.h[:, off:off + units]
        if dt == F32:
            v = v.bitcast(F32)
        if len(free) > 1:
            names = "abcdefg"[:len(free)]
            pat = "p (" + " ".join(names) + ") -> p " + " ".join(names)
            v = v.rearrange(pat, **{k: int(s) for k, s in zip(names, free)})
        return v


def build_program(stop_after=None):
    nc = bass.Bass("TRN2", target_bir_lowering=False)

    def din(name, shape):
        return nc.dram_tensor(name, list(shape), F32, kind="ExternalInput").ap()

    x_d = din("x", [SEQ, DM])
    xT_d = din("xT", [DM, SEQ])
    memT_d = din("memT", [DM, 256])
    biasA_d = din("biasA", [128, 12, 256])
    ident_d = din("ident", [128, 128])
    w_in_d = din("w_in", [DM, 3584])
    wqkv_d = din("w_qkv_r", [DM, 6, 384])
    w_kv_d = din("w_mem_kv", [DM, 512])
    sgg_d = din("sg_ln_g", [1, 512])
    sgbT_d = din("sg_ln_bT", [128, 4])
    wspT_d = din("wspT", [128, 8, 128])
    trilT_d = din("trilT", [128, 128])
    bsp_d = din("bspT", [128, 4, 128])
    wpa_d = din("w_proj_a", [256, DM])
    wpb_d = din("w_proj_b", [512, DM])
    wpc_d = din("w_proj_c", [256, DM])
    wg_d = din("w_gate_r", [DM, 8, 384])
    bg_d = din("b_gateT", [128, 24])
    wo_d = din("w_out", [DM, DM])
    ln1g_d = din("ln1_g", [1, DM])
    ln1b_d = din("ln1_b", [1, DM])
    wup_d = din("w_ffn_up", [DM, 2 * D_FF])
    cw_d = din("conv_wT", [128, NJ, 3])
    cb_d = din("conv_bT", [128, NJ])
    wdn_d = din("w_ffn_down", [D_FF, DM])
    ln2g_d = din("ln2_g", [1, DM])
    ln2b_d = din("ln2_b", [1, DM])
    out_d = nc.dram_tensor("out", [SEQ, DM], F32, kind="ExternalOutput").ap()
    h1_d = nc.dram_tensor("h1_scratch", [SEQ, DM], F32, kind="ExternalOutput").ap()
    dbg_d = None
    if stop_after == "A":
        dbg_d = nc.dram_tensor("dbg", [128, 2, SEQ], F32, kind="ExternalOutput").ap()

    with ExitStack() as stack:
        arena = Arena(nc, stack, 106368)
        ps_h = stack.enter_context(nc.psum_tensor("ps", [128, 8, 512], F32))
        ps = ps_h[:]
        bank = [ps[:, i, :] for i in range(8)]
        bres = [Res("bank%d" % i) for i in range(8)]
        S = Sched(nc, stack)

        ident_f = arena.alloc([128], F32)
        ident_b = arena.alloc([128], BF16)
        eps_t = arena.alloc([1], F32)
        mhalf_t = arena.alloc([1], F32)
        neg1_t = arena.alloc([1], F32)
        r_const = Res("const")
        S.op("sp", lambda e: e.dma_start(out=ident_f, in_=ident_d[:, :]), writes=[r_const], dma="c0")
        S.op("pool", lambda e: e.dma_start(out=ident_b, in_=ident_d[:, :]), writes=[r_const], dma="c1")
        S.op("dve", lambda e: e.memset(eps_t, LN_EPS), writes=[r_const])
        S.op("dve", lambda e: e.memset(mhalf_t, -0.5), writes=[r_const])
        S.op("dve", lambda e: e.memset(neg1_t, -1.0), writes=[r_const])
        ln_slots = []
        for i in range(4):
            ln_slots.append((arena.alloc([2, 6], F32), arena.alloc([2], F32), arena.alloc([1], F32), arena.alloc([1], F32), Res("ln%d" % i)))
        const_mark = arena.top
        oaT = arena.alloc([2, SEQ], BF16)
        oares = [Res("oa0"), Res("oa1")]
        kcT = arena.alloc([4, 256], BF16)
        vca = arena.alloc([2, 2, 3, 64], BF16)
        wspT = arena.alloc([8, 128], BF16)
        r_kv = Res("kv")
        r_wsp = Res("wsp")
        r_min = Res("min")
        w_bc = arena.alloc([KC, 1280], BF16)
        r_wbc = Res("wbc")
        base_mark = arena.top
        mark_T = base_mark

        xT_v = xT_d.rearrange("(kc p) t -> p kc t", p=128)
        w_in_v = w_in_d.rearrange("(kc p) n -> p kc n", p=128)

        def phase_A():
            xTb = arena.alloc([KC, SEQ], BF16)
            xres = [Res("xT%d" % t) for t in range(NT)]
            wslab = [arena.alloc([KC, 3, 128], BF16) for _ in range(2)]
            wsl = {n: [wslab[s_][:, :, i, :] for s_ in range(2)] for i, n in enumerate("qkv")}
            wslres = [Res("wslab0"), Res("wslab1")]
            wres = {n: wslres for n in "qkv"}
            wqkv_v = wqkv_d.rearrange("(k p) i n -> p k i n", p=128)

            def load_w(it):
                slot = it % 2
                S.op("pool", lambda e, it=it, slot=slot: e.dma_start(
                    out=wslab[slot].rearrange("p k i n -> p k (i n)"), in_=wqkv_v[:, :, it, :]),
                    writes=[wslres[slot]], dma="wslab%d" % slot)

            load_w(0)
            for t in range(NT):
                S.op("pool", lambda e, t=t: e.dma_start(out=xTb[:, :, t * TT:(t + 1) * TT],
                                                        in_=xT_v[:, :, t * TT:(t + 1) * TT]),
                     writes=[xres[t]], dma="xT%d" % t)
            bst = arena.alloc([12, 256], F32)
            r_b = Res("bias")
            S.op("sp", lambda e: e.dma_start(out=bst, in_=biasA_d[:, :, :]), writes=[r_b], dma="b0")
            sbt = [arena.alloc([256], F32) for _ in range(4)]
            sbres = [Res() for _ in range(4)]
            S.op("pool", lambda e: e.dma_start(out=w_bc, in_=w_in_v[:, :, 2304:3584]), writes=[r_wbc], dma="t0")
            m_state = [False]

            qT = arena.alloc([SEQ], BF16)
            kTz = [arena.alloc([SEQ], BF16) for _ in range(2)]
            r_kz = Res("kz")
            S.op("pool", lambda e: e.memset(kTz[0][64:128, :], 0.0), writes=[r_kz])
            S.op("pool", lambda e: e.memset(kTz[1][0:64, :], 0.0), writes=[r_kz])
            qres = [Res() for _ in range(NT)]
            kres = [Res() for _ in range(NT)]
            V = arena.alloc([32, 3, 64], BF16)
            vres = [Res() for _ in range(8)]
            r_ones = Res("ones")
            S.op("pool", lambda e: e.memset(V[:, :, 1, :], 1.0), writes=[r_ones])
            acc = []
            acc_off = []
            for _ in range(2):
                acc.append(arena.alloc([SEQ], F32))
                acc_off.append(arena.last_off)
            accres = [Res("acc0"), Res("acc1")]
            o0, o1 = acc_off
            m_memTb = arena.h[:, o0:o0 + 2048].rearrange("p (k m) -> p k m", k=KC)
            m_wkv = arena.h[:, o0 + 2048:o0 + 6144].rearrange("p (k n) -> p k n", k=KC)
            m_wsp_f = arena.h[:, o1:o1 + 2048].bitcast(F32).rearrange("p (g q) -> p g q", g=8)
            m_tril_f = arena.h[:, o1 + 2048:o1 + 2304].bitcast(F32)
            m_args = (m_memTb, m_wkv, m_wsp_f, m_tril_f, accres[0], accres[1])
            recs = [arena.alloc([256], F32) for _ in range(1)]
            recress = [Res("rec0"), Res("rec1")]
            PT = [arena.alloc([256], BF16) for _ in range(6)]
            ptres = [Res() for _ in range(6)]
            st_slots = [(bank[i][:, 0:256], bres[i]) for i in range(4)]
            obanks = [(bank[4], bres[4]), (bank[5], bres[5])]
            pbanks = [(bank[6 + i], bres[6 + i]) for i in range(2)]
            pb_i = [0]

            def next_pb():
                b = pbanks[pb_i[0] % 2]
                pb_i[0] += 1
                return b

            S.op("pool", lambda e: e.dma_start(out=w_bc, in_=w_in_v[:, :, 2304:3584]), writes=[r_wbc], dma="t0")
            m_state = [False]
            pending = []

            def finalize_ops(p, hh):
                ops = []
                nr = slice(hh * 64, hh * 64 + 64)
                dr = slice((1 - hh) * 64, (1 - hh) * 64 + 64)
                for c in range(16):
                    cs = slice(c * 256, (c + 1) * 256)

                    def f(nr=nr, dr=dr, cs=cs, hh=hh, p=p, c=c):
                        rec, recres = recs[0], recress[0]
                        S.op("dve", lambda e: e.reciprocal(out=rec[nr, :], in_=acc[hh][dr, cs]),
                             reads=[accres[hh]], writes=[recres])
                        S.op("dve", lambda e: e.tensor_tensor(
                            out=oaT[nr, p, cs], in0=acc[hh][nr, cs], in1=rec[nr, :], op=ALU.mult),
                            reads=[accres[hh], recres], writes=[oares[p]])
                    ops.append(f)
                return ops

            st_i = [0]
            ob_i = [0]
            for it in range(6):
                p, g = divmod(it, 3)
                slot = it % 2
                d = DIL[g]
                nbr = 32 // d
                if it == 0:
                    phase_M_loads(*m_args)
                if it + 1 < 6:
                    load_w(it + 1)
                for t in range(NT):
                    for n in ("q", "k"):
                        pb, pr = next_pb()

                        def mm(e, n=n, t=t, pb=pb, slot=slot):
                            last = None
                            for kc in range(KC):
                                last = e.matmul(pb, wsl[n][slot][:, kc, :], xTb[:, kc, t * TT:(t + 1) * TT],
                                                start=(kc == 0), stop=(kc == KC - 1))
                            return last
                        S.op("pe", mm, reads=[wres[n][slot], xres[t]], writes=[pr])
                        w_ = TT // d
                        if n == "q":
                            S.op("act", lambda e, t=t, pb=pb, d=d, w_=w_: e.mul(
                                qT.rearrange("p (r i) -> p r i", r=d)[:, :, t * w_:(t + 1) * w_],
                                pb.rearrange("p (i r) -> p r i", r=d), 0.125),
                                reads=[pr], writes=[qres[t]])
                        else:
                            def kcopy(e, t=t, pb=pb, d=d, w_=w_):
                                src = pb.rearrange("p (i r) -> p r i", r=d)
                                e.copy(out=kTz[0].rearrange("p (r i) -> p r i", r=d)[0:64, :, t * w_:(t + 1) * w_], in_=src[0:64])
                                return e.copy(out=kTz[1].rearrange("p (r i) -> p r i", r=d)[64:128, :, t * w_:(t + 1) * w_], in_=src[64:128])
                            S.op("act", kcopy, reads=[pr], writes=[kres[t]], attach=False)
                            if pending:
                                pending.pop(0)()
                xv = xTb.rearrange("p k (i d) -> p k d i", d=d)
                for bb in range(8):
                    pb, pr = next_pb()

                    def mmv(e, bb=bb, pb=pb, slot=slot, xv=xv, nbr=nbr):
                        last = None
                        for j in range(4):
                            blk = bb * 4 + j
                            r, n = divmod(blk, nbr)
                            for kc in range(KC):
                                last = e.matmul(pb[:, j * 128:(j + 1) * 128],
                                                xv[:, kc, r, n * 128:(n + 1) * 128],
                                                wsl["v"][slot][:, kc, :],
                                                start=(kc == 0), stop=(kc == KC - 1))
                        return last
                    need = set()
                    for j in range(4):
                        r_, n_ = divmod(bb * 4 + j, nbr)
                        lo_t = n_ * 128 * d + r_
                        hi_t = (n_ * 128 + 127) * d + r_
                        need.update(range(lo_t // TT, hi_t // TT + 1))
                    S.op("pe", mmv, reads=[wres["v"][slot]] + [xres[i] for i in sorted(need)], writes=[pr])
                    S.op("act", lambda e, bb=bb, pb=pb: e.copy(
                        out=V[:, bb * 4:(bb + 1) * 4, 0:3:2, :],
                        in_=pb.rearrange("p (a b c) -> p a b c", a=4, b=2, c=64)),
                        reads=[pr], writes=[vres[bb]])
                if it == 0:
                    phase_M_compute(*m_args, pbanks)
                qv = qT.rearrange("p (r i) -> p r i", r=d)
                kvz = [kTz[h_].rearrange("p (r i) -> p r i", r=d) for h_ in range(2)]
                for hh in range(2):
                    hg = g * 4 + p * 2 + hh
                    lo_, hi_ = hh * 64, hh * 64 + 64
                    accv = acc[hh].rearrange("p (i d) -> p d i", d=d)
                    steps = [(r, j) for r in range(d) for j in range(nbr)]
                    info = {}

                    def emitS(i, steps=steps, qv=qv, kv=kvz[hh], hg=hg, lo_=lo_, hi_=hi_, nbr=nbr, info=info):
                        r, j = steps[i]
                        nq = 256 if j + 1 < nbr else 128
                        sap, sres = st_slots[st_i[0] % 4]
                        st_i[0] += 1
                        pt_idx = i % 6

                        def mm(e):
                            return e.matmul(sap[:, :nq], kv[:, r, j * 128:(j + 1) * 128],
                                            qv[:, r, j * 128:j * 128 + nq], start=True, stop=True)
                        S.op("pe", mm, reads=qres + kres + [r_kz], writes=[sres])
                        sb_idx = i % 4
                        sb = sbt[sb_idx]
                        S.op("dve", lambda e: e.tensor_tensor(out=sb[:, :nq], in0=sap[:, :nq], in1=bst[:, hg, :nq], op=ALU.add),
                             reads=[sres, r_b], writes=[sbres[sb_idx]])
                        S.op("act", lambda e: e.activation(out=PT[pt_idx][:, :nq], in_=sb[:, :nq], func=AF.Exp),
                             reads=[sbres[sb_idx]], writes=[ptres[pt_idx]])
                        info[i] = pt_idx

                    def emitPV(i, steps=steps, hh=hh, nbr=nbr, accv=accv, info=info, g=g, d=d):
                        r, j = steps[i]
                        pos = i % 4
                        oap, ores = obanks[ob_i[0] % 2]
                        blk = r * nbr + j
                        sl = slice(0, 2) if hh == 0 else slice(1, 3)

                        def mm(e):
                            if j > 0:
                                e.matmul(oap[:, pos * 128:(pos + 1) * 128],
                                         V[:, blk - 1, sl, :].rearrange("p a b -> p (a b)"),
                                         PT[info[i - 1]][:, 128:256], start=True, stop=False)
                            return e.matmul(oap[:, pos * 128:(pos + 1) * 128],
                                            V[:, blk, sl, :].rearrange("p a b -> p (a b)"),
                                            PT[info[i]][:, 0:128], start=(j == 0), stop=True)
                        rd = [ptres[info[i]], vres[blk // 4], r_ones]
                        if j > 0:
                            rd += [ptres[info[i - 1]], vres[(blk - 1) // 4]]
                        S.op("pe", mm, reads=rd, writes=[ores])
                        if pos == 3:
                            m = i // 4
                            if nbr >= 4:
                                r0, n0 = divmod(4 * m, nbr)
                                dst = accv[:, r0, n0 * 128:(n0 + 4) * 128]
                                src = oap
                            else:
                                r0 = (4 * m) // nbr
                                dst = accv[:, r0:r0 + 2, :]
                                src = oap.rearrange("p (a b) -> p a b", a=2)
                            if g == 0:
                                S.op("dve", lambda e: e.tensor_copy(out=dst, in_=src),
                                     reads=[ores], writes=[accres[hh]])
                            else:
                                S.op("dve", lambda e: e.tensor_tensor(out=dst, in0=src, in1=dst, op=ALU.add),
                                     reads=[ores], writes=[accres[hh]])
                            ob_i[0] += 1
                            for _ in range(2):
                                if pending:
                                    pending.pop(0)()

                    emitS(0)
                    emitS(1)
                    emitS(2)
                    for i in range(len(steps)):
                        if i + 3 < len(steps):
                            emitS(i + 3)
                        emitPV(i)
                    if g == 2:
                        fo = finalize_ops(p, hh)
                        pending.extend(fo)
                        if hh == 1 and it == 5:
                            while pending:
                                pending.pop(0)()

        bk_i = [0]

        def next_bank(lo=0, n=8):
            i = lo + bk_i[0] % n
            bk_i[0] += 1
            return bank[i], bres[i]

        def mm_acc(pb, pairs):
            def f(e):
                last = None
                for i, (l, r) in enumerate(pairs):
                    last = e.matmul(pb, l, r, start=(i == 0), stop=(i == len(pairs) - 1))
                return last
            return f

        ln_i = [0]

        def ln_norm(zt, zres, width):
            st6, mv, rstd, nmr, r_ln = ln_slots[ln_i[0] % 4]
            ln_i[0] += 1
            nch = width // 512

            def stats(e):
                last = None
                for c in range(nch):
                    last = e.bn_stats(out=st6[:, c, :], in_=zt[:, c * 512:(c + 1) * 512])
                return last
            S.op("dve", stats, reads=[zres], writes=[r_ln], attach=False)
            S.op("dve", lambda e: e.bn_aggr(out=mv, in_=st6[:, 0:nch, :].rearrange("p a b -> p (a b)")),
                 reads=[r_ln], writes=[r_ln])
            S.op("pool", lambda e: e.tensor_tensor(out=rstd, in0=mv[:, 1:2], in1=eps_t, op=ALU.add),
                 reads=[r_ln, r_const], writes=[r_ln])
            S.op("pool", lambda e: e.tensor_tensor(out=rstd, in0=rstd, in1=mhalf_t, op=ALU.pow),
                 reads=[r_ln, r_const], writes=[r_ln])
            S.op("pool", lambda e: e.tensor_tensor(out=nmr, in0=mv[:, 0:1], in1=rstd, op=ALU.mult),
                 reads=[r_ln], writes=[r_ln])
            S.op("pool", lambda e: e.tensor_tensor(out=nmr, in0=nmr, in1=neg1_t, op=ALU.mult),
                 reads=[r_ln, r_const], writes=[r_ln])
            S.op("act", lambda e: e.activation(out=zt, in_=zt, func=AF.Identity, bias=nmr[:, 0:1], scale=rstd[:, 0:1]),
                 reads=[r_ln, zres], writes=[zres])

        def ln_affine(zt, zres, gt, bt, r_gb, out_t, out_res, aff):
            wr = [out_res] if out_res is zres else [out_res, zres]
            if bt is None:
                S.op(aff, lambda e: e.tensor_tensor(out=out_t, in0=zt, in1=gt, op=ALU.mult),
                     reads=[zres, r_gb], writes=wr)
                return
            S.op(aff, lambda e: e.tensor_tensor(out=zt, in0=zt, in1=gt, op=ALU.mult),
                 reads=[zres, r_gb], writes=[zres])
            S.op(aff, lambda e: e.tensor_tensor(out=out_t, in0=zt, in1=bt, op=ALU.add),
                 reads=[zres, r_gb], writes=wr)


        def phase_M_loads(memTb, wkv, wsp_f, tril_f, g0, g1):
            S.op("pool", lambda e: e.dma_start(out=memTb, in_=memT_d.rearrange("(k p) m -> p k m", p=128)),
                 writes=[r_min, g0], dma="m0")
            S.op("pool", lambda e: e.dma_start(out=wkv, in_=w_kv_d.rearrange("(k p) n -> p k n", p=128)),
                 writes=[r_min, g0], dma="m1")
            S.op("sp", lambda e: e.dma_start(out=wsp_f, in_=wspT_d[:, :, :]), writes=[r_wsp, g1], dma="m2")
            S.op("sp", lambda e: e.dma_start(out=tril_f, in_=trilT_d[:, :]), writes=[r_wsp, g1], dma="m3")

        def phase_M_compute(memTb, wkv, wsp_f, tril_f, g0, g1, banks):
            S.op("dve", lambda e: e.tensor_tensor(out=wspT, in0=wsp_f,
                                                  in1=tril_f.unsqueeze(1).broadcast_to([128, 8, 128]), op=ALU.mult),
                 reads=[r_wsp, g1], writes=[r_wsp])
            S.op("pool", lambda e: e.memset(vca[:, :, :, 1, :], 1.0), writes=[r_kv])
            S.op("pool", lambda e: e.memset(kcT, 0.0), writes=[r_kv])
            bi = 0
            for c in range(2):
                pb, pr = banks[bi % len(banks)]
                bi += 1
                S.op("pe", mm_acc(pb[:, 0:256], [(wkv[:, k, c * 128:(c + 1) * 128], memTb[:, k, :]) for k in range(KC)]),
                     reads=[r_min, g0], writes=[pr])

                def kccopy(e, c=c, pb=pb):
                    e.tensor_copy(out=kcT[0:64, 2 * c, :], in_=pb[0:64, 0:256])
                    return e.tensor_copy(out=kcT[64:128, 2 * c + 1, :], in_=pb[64:128, 0:256])
                S.op("dve", kccopy, reads=[pr], writes=[r_kv], attach=False)
            for mc in range(2):
                pb, pr = banks[bi % len(banks)]
                bi += 1
                S.op("pe", mm_acc(pb[:, 0:256], [(memTb[:, k, mc * 128:(mc + 1) * 128], wkv[:, k, 256:512]) for k in range(KC)]),
                     reads=[r_min, g0], writes=[pr])
                S.op("dve", lambda e, mc=mc, pb=pb: e.tensor_copy(
                    out=vca[:, mc, :, 0:3:2, :],
                    in_=pb[:, 0:256].rearrange("p (a b c) -> p a b c", a=2, b=2, c=64)),
                    reads=[pr], writes=[r_kv])

        phase_A()
        if stop_after == "A":
            dbg_t = arena.alloc([SEQ], F32)
            r_dbg = Res("dbg")
            r_o = Res("o")
            for c in range(2):
                S.op("dve", lambda e, c=c: e.tensor_copy(out=dbg_t, in_=oaT[:, c, :]), reads=oares, writes=[r_dbg])
                S.op("sp", lambda e, c=c: e.dma_start(out=dbg_d[:, c, :], in_=dbg_t), reads=[r_dbg], writes=[r_o], dma="o")
            S.wait_all("sp", [r_o])
            with nc.Block() as block:
                S.emit_all(block)
            return nc

        h1res = [[Res() for _ in range(4)] for _ in range(NT)]

        jch = [(0, 6), (6, 12), (12, 18), (18, 22)]
        wup_v = wup_d.rearrange("(k p) n -> p k n", p=128)
        _save_top = arena.top
        arena.top = const_mark
        w_up_a, w_up_b = [], []
        for (j0, j1) in jch:
            w_up_a.append(arena.alloc([KC, (j1 - j0) * 128], BF16))
            w_up_b.append(arena.alloc([KC, (j1 - j0) * 128], BF16))
        f_base = arena.top
        arena.top = _save_top
        upres_a = [Res() for _ in jch]
        upres_b = [Res() for _ in jch]

        def jchunk(j):
            for ci, (j0, j1) in enumerate(jch):
                if j0 <= j < j1:
                    return ci

        def load_up_chunk(ci, guards=()):
            j0, j1 = jch[ci]
            for part, tiles, rr in ((0, w_up_a, upres_a), (1, w_up_b, upres_b)):
                c0, c1 = part * D_FF + j0 * 128, part * D_FF + j1 * 128
                S.op("pool", lambda e, c0=c0, c1=c1, t_=tiles[ci]: e.dma_start(out=t_, in_=wup_v[:, :, c0:c1]),
                     writes=[rr[ci]] + list(guards), dma="f_up%d_%d" % (part, ci))

        def phase_T():
            w_pa = arena.alloc([2, DM], BF16)
            w_pb = arena.alloc([4, DM], BF16)
            w_pc = arena.alloc([2, DM], BF16)
            w_o = arena.alloc([KC, DM], BF16)
            r_wp, r_wo, r_c = Res(), Res(), Res("cT")
            ln1g = arena.alloc([DM], F32)
            ln1b = arena.alloc([DM], F32)
            sgg = arena.alloc([512], F32)
            sgbT = arena.alloc([4], F32)
            bsp = arena.alloc([512], F32)
            bg = arena.alloc([24], F32)
            S.op("sp", lambda e: e.dma_start(out=ln1g, in_=ln1g_d[0:1, :].broadcast_to([128, DM])), writes=[r_c], dma="t1")
            S.op("sp", lambda e: e.dma_start(out=ln1b, in_=ln1b_d[0:1, :].broadcast_to([128, DM])), writes=[r_c], dma="t2")
            S.op("sp", lambda e: e.dma_start(out=sgg, in_=sgg_d[0:1, :].broadcast_to([128, 512])), writes=[r_c], dma="t3")
            S.op("sp", lambda e: e.dma_start(out=sgbT, in_=sgbT_d[:, :]), writes=[r_c], dma="t4")
            S.op("sp", lambda e: e.dma_start(out=bsp, in_=bsp_d.rearrange("p a b -> p (a b)")), writes=[r_c], dma="t5")
            S.op("sp", lambda e: e.dma_start(out=bg, in_=bg_d[:, :]), writes=[r_c], dma="t6")

            xt = [arena.alloc([KC, TT], BF16) for _ in range(2)]
            xtres = [Res(), Res()]
            qcT = arena.alloc([2, TT], BF16)
            qcres = [Res(), Res()]
            ptC = [arena.alloc([TT], BF16) for _ in range(8)]
            ptCres = [Res() for _ in range(8)]
            ocT = arena.alloc([2, TT], BF16)
            ocres = [Res(), Res()]
            recC = arena.alloc([TT], F32)
            r_recC = Res()
            uT = arena.alloc([4, TT], F32)
            ures = [Res() for _ in range(4)]
            vgf = [arena.alloc([512], F32) for _ in range(4)]
            vgfres = [Res() for _ in range(4)]
            vgn = [arena.alloc([512], BF16) for _ in range(4)]
            vgnres = [Res() for _ in range(4)]
            tmpB = arena.alloc([512], F32)
            r_tmpB = Res()
            obT = arena.alloc([4, TT], BF16)
            r_obT = Res()
            wgs = [arena.alloc([KC, 3, 128], BF16) for _ in range(3)]
            wgres = [Res(), Res(), Res()]
            gate = [arena.alloc([TT], F32) for _ in range(6)]
            gres = [Res() for _ in range(6)]
            mt = [arena.alloc([TT], F32) for _ in range(3)]
            mres = [Res() for _ in range(3)]
            mergedT = arena.alloc([KC, TT], BF16)
            mgres = [Res() for _ in range(KC)]
            zt = [arena.alloc([DM], F32) for _ in range(4)]
            zres = [Res() for _ in range(4)]

            def load_wp():
                S.op("pool", lambda e: e.dma_start(out=w_pa, in_=wpa_d.rearrange("(c p) n -> p c n", p=128)), writes=[r_wp], dma="t7")
                S.op("pool", lambda e: e.dma_start(out=w_pb, in_=wpb_d.rearrange("(c p) n -> p c n", p=128)), writes=[r_wp], dma="t8")
                S.op("pool", lambda e: e.dma_start(out=w_pc, in_=wpc_d.rearrange("(c p) n -> p c n", p=128)), writes=[r_wp], dma="t9")

            def load_wo():
                S.op("pool", lambda e: e.dma_start(out=w_o, in_=wo_d.rearrange("(c p) n -> p c n", p=128)), writes=[r_wo], dma="t10")

            xt0_early = [False]

            def prep_bsp():
                ones_b = arena.alloc([64], BF16)
                S.op("dve", lambda e: e.memset(ones_b, 1.0), writes=[r_c])
                pb0, pr0 = next_bank()

                def mmrs(e):
                    last = None
                    for g8 in range(8):
                        gp, gl = divmod(g8, 2)
                        last = e.matmul(pb0[gl * 64:(gl + 1) * 64, gp * 128:(gp + 1) * 128],
                                        ones_b, wspT[:, g8, :], start=True, stop=True)
                    return last
                S.op("pe", mmrs, reads=[r_c, r_wsp], writes=[pr0])
                for gp in range(4):
                    S.op("dve", lambda e, gp=gp: e.scalar_tensor_tensor(
                        out=bsp[:, gp * 128:(gp + 1) * 128], in0=pb0[:, gp * 128:(gp + 1) * 128],
                        scalar=sgbT[:, gp:gp + 1], in1=bsp[:, gp * 128:(gp + 1) * 128], op0=ALU.mult, op1=ALU.add),
                        reads=[pr0, r_c], writes=[r_c])


            wg_v = wg_d.rearrange("(k p) d n -> p k d n", p=128)

            def load_xt(t):
                S.op("pool", lambda e, t=t: e.dma_start(out=xt[t % 2], in_=xT_v[:, :, t * TT:(t + 1) * TT]),
                     writes=[xtres[t % 2]], dma="xt%d" % (t % 2))

            def load_wg(gi):
                dmc = gi % KC
                S.op("pool", lambda e, gi=gi, dmc=dmc: e.dma_start(
                    out=wgs[gi % 3].rearrange("p k b n -> p k (b n)"), in_=wg_v[:, :, dmc, :]),
                    writes=[wgres[gi % 3]], dma="wg%d" % (gi % 3))

            load_xt(0)

            z_i = [0]
            g_i = [0]

            def stage_C1(t):
                xs, xr = xt[t % 2], xtres[t % 2]
                for c in range(2):
                    pb, pr = next_bank()
                    S.op("pe", mm_acc(pb, [(w_bc[:, k, 1024 + c * 128:1024 + (c + 1) * 128], xs[:, k, :]) for k in range(KC)]),
                         reads=[r_wbc, xr], writes=[pr])
                    S.op("act", lambda e, c=c, pb=pb: e.mul(qcT[:, c, :], pb, 0.125), reads=[pr], writes=[qcres[c]])
                for h in range(4):
                    c, hh = divmod(h, 2)
                    nr = slice(hh * 64, hh * 64 + 64)
                    for mc in range(2):
                        pb, pr = next_bank()
                        S.op("pe", mm_acc(pb, [(kcT[:, h, mc * 128:(mc + 1) * 128], qcT[:, c, :])]),
                             reads=[r_kv, qcres[c]], writes=[pr])
                        k_ = h * 2 + mc
                        S.op("act", lambda e, k_=k_, pb=pb: e.activation(out=ptC[k_], in_=pb, func=AF.Exp),
                             reads=[pr], writes=[ptCres[k_]])

            def stage_C2(t):
                for h in range(4):
                    c, hh = divmod(h, 2)
                    nr = slice(hh * 64, hh * 64 + 64)
                    dr = slice((1 - hh) * 64, (1 - hh) * 64 + 64)
                    sl2 = slice(0, 2) if hh == 0 else slice(1, 3)
                    ob, orr = next_bank()
                    S.op("pe", mm_acc(ob, [(vca[:, mc, c, sl2, :].rearrange("p a b -> p (a b)"), ptC[h * 2 + mc]) for mc in range(2)]),
                         reads=[r_kv, ptCres[h * 2], ptCres[h * 2 + 1]], writes=[orr])
                    S.op("dve", lambda e, nr=nr, dr=dr, ob=ob: e.reciprocal(out=recC[nr, :], in_=ob[dr, :]),
                         reads=[orr], writes=[r_recC])
                    S.op("dve", lambda e, nr=nr, c=c, ob=ob: e.tensor_tensor(out=ocT[nr, c, :], in0=ob[nr, :], in1=recC[nr, :], op=ALU.mult),
                         reads=[orr, r_recC], writes=[ocres[c]])

            def stage_B1(t):
                xs, xr = xt[t % 2], xtres[t % 2]
                for fc in range(4):
                    pb, pr = next_bank()
                    S.op("pe", mm_acc(pb, [(w_bc[:, k, fc * 128:(fc + 1) * 128], xs[:, k, :]) for k in range(KC)]),
                         reads=[r_wbc, xr], writes=[pr])
                    S.op("act", lambda e, fc=fc, pb=pb: e.activation(out=uT[:, fc, :], in_=pb, func=AF.Gelu_apprx_tanh),
                         reads=[pr], writes=[ures[fc]])
                for sub in range(4):
                    ts_ = slice(sub * 128, (sub + 1) * 128)
                    pb, pr = next_bank()
                    S.op("pe", mm_acc(pb, [(xs[:, k, ts_], w_bc[:, k, 512:1024]) for k in range(KC)]),
                         reads=[r_wbc, xr], writes=[pr])
                    S.op("act", lambda e, sub=sub, pb=pb: e.activation(out=vgf[sub], in_=pb, func=AF.Gelu_apprx_tanh),
                         reads=[pr], writes=[vgfres[sub]])
                for sub in range(4):
                    ln_norm(vgf[sub], vgfres[sub], 512)
                for sub in range(4):
                    ln_affine(vgf[sub], vgfres[sub], sgg, None, r_c, vgn[sub], vgnres[sub], "pool")

            def stage_B2(t):
                for sub in range(4):
                    ts_ = slice(sub * 128, (sub + 1) * 128)
                    pb2, pr2 = next_bank()

                    def mmsv(e, pb2=pb2, sub=sub):
                        last = None
                        for g8 in range(8):
                            gp, gl = divmod(g8, 2)
                            last = e.matmul(pb2[gl * 64:(gl + 1) * 64, gp * 128:(gp + 1) * 128],
                                            vgn[sub][:, g8 * 64:(g8 + 1) * 64], wspT[:, g8, :], start=True, stop=True)
                        return last
                    S.op("pe", mmsv, reads=[vgnres[sub], r_wsp], writes=[pr2])
                    S.op("dve", lambda e, pb2=pb2: e.tensor_tensor(out=tmpB, in0=pb2, in1=bsp, op=ALU.add),
                         reads=[pr2, r_c], writes=[r_tmpB])
                    S.op("dve", lambda e, ts_=ts_: e.tensor_tensor(
                        out=obT[:, :, ts_], in0=tmpB.rearrange("p (a b) -> p a b", a=4), in1=uT[:, :, ts_], op=ALU.mult),
                        reads=[r_tmpB] + ures, writes=[r_obT])

            def stage_G(t):
                xs, xr = xt[t % 2], xtres[t % 2]
                tok = slice(t * TT, (t + 1) * TT)
                gsets = {}

                def gates(dmc):
                    gi = g_i[0]
                    g_i[0] += 1
                    ws = wgs[gi % 3]
                    gts = []
                    for b in range(3):
                        pb, pr = next_bank()
                        S.op("pe", mm_acc(pb, [(ws[:, k, b, :], xs[:, k, :]) for k in range(KC)]),
                             reads=[wgres[gi % 3], xr], writes=[pr])
                        k_ = (gi % 2) * 3 + b
                        S.op("act", lambda e, k_=k_, pb=pb, b=b, dmc=dmc: e.activation(
                            out=gate[k_], in_=pb, func=AF.Sigmoid, bias=bg[:, b * 8 + dmc:b * 8 + dmc + 1], scale=1.0),
                            reads=[pr, r_c], writes=[gres[k_]])
                        gts.append(k_)
                    if gi + 3 < NT * KC:
                        load_wg(gi + 3)
                    gsets[dmc] = gts

                def proj(dmc):
                    fs = slice(dmc * 128, (dmc + 1) * 128)
                    gts = gsets[dmc]
                    srcs = [(w_pa, 2, lambda c: oaT[:, c, tok], oares),
                            (w_pb, 4, lambda c: obT[:, c, :], [r_obT]),
                            (w_pc, 2, lambda c: ocT[:, c, :], ocres)]
                    for b in range(3):
                        wp, ncn, rhs_f, rr = srcs[b]
                        pb, pr = next_bank()
                        S.op("pe", mm_acc(pb, [(wp[:, c, fs], rhs_f(c)) for c in range(ncn)]),
                             reads=[r_wp] + list(rr), writes=[pr])
                        S.op("dve", lambda e, b=b, pb=pb, k_=gts[b]: e.tensor_tensor(out=mt[b], in0=pb, in1=gate[k_], op=ALU.mult),
                             reads=[pr, gres[gts[b]]], writes=[mres[b]])
                    S.op("dve", lambda e: e.tensor_tensor(out=mt[0], in0=mt[0], in1=mt[1], op=ALU.add),
                         reads=[mres[1]], writes=[mres[0]])
                    S.op("dve", lambda e, dmc=dmc: e.tensor_tensor(out=mergedT[:, dmc, :], in0=mt[0], in1=mt[2], op=ALU.add),
                         reads=[mres[0], mres[2]], writes=[mgres[dmc]])

                for dmc in range(KC + 1):
                    if dmc < KC:
                        gates(dmc)
                    if dmc >= 1:
                        proj(dmc - 1)

            def load_x(t):
                for sub in range(4):
                    row0 = t * TT + sub * 128
                    S.op("sp", lambda e, sub=sub, row0=row0: e.dma_start(out=zt[sub], in_=x_d[row0:row0 + 128, :]),
                         writes=[zres[sub]], dma="xz%d" % sub)

            def stage_Y(t):
                for sub in range(4):
                    ts_ = slice(sub * 128, (sub + 1) * 128)
                    z_ = sub
                    row0 = t * TT + sub * 128
                    for half in range(2):
                        hs_ = slice(half * 512, (half + 1) * 512)
                        pb, pr = next_bank()
                        S.op("pe", mm_acc(pb, [(mergedT[:, k, ts_], w_o[:, k, hs_]) for k in range(KC)]),
                             reads=mgres + [r_wo], writes=[pr])
                        S.op("dve", lambda e, z_=z_, hs_=hs_, pb=pb: e.scalar_tensor_tensor(
                            out=zt[z_][:, hs_], in0=zt[z_][:, hs_], scalar=ALPHA, in1=pb, op0=ALU.mult, op1=ALU.add),
                            reads=[pr], writes=[zres[z_]])
                    ln_norm(zt[z_], zres[z_], DM)
                for sub in range(4):
                    z_ = sub
                    row0 = t * TT + sub * 128
                    ln_affine(zt[z_], zres[z_], ln1g, ln1b, r_c, zt[z_], zres[z_], "pool")
                    S.op("pool", lambda e, z_=z_, row0=row0: e.dma_start(out=h1_d[row0:row0 + 128, :], in_=zt[z_]),
                         reads=[zres[z_]], writes=[h1res[t][sub]], dma="h1w%d" % z_)
                if t + 1 < NT:
                    load_x(t + 1)

            load_x(0)
            load_wg(0)
            load_wp()
            load_wg(1)
            load_wg(2)
            load_xt(1)
            load_wo()
            stage_C1(0)
            stage_B1(0)
            prep_bsp()
            for t in range(NT):
                stage_C2(t)
                stage_B2(t)
                stage_G(t)
                if t == NT - 1:
                    load_up_chunk(0, guards=oares + [r_kv, r_wsp, r_wbc])
                if t + 1 < NT:
                    if t + 2 < NT:
                        load_xt(t + 2)
                    stage_C1(t + 1)
                    stage_B1(t + 1)
                stage_Y(t)

        outres = Res("out")

        def phase_F():
            w_dn = arena.alloc([NJ, DM], BF16)
            wdn_v = wdn_d.rearrange("(j p) n -> p j n", p=128)
            dnres = [Res() for _ in range(2)]
            for ci in range(1, len(jch)):
                load_up_chunk(ci)
            for q in range(2):
                S.op("pool", lambda e, q=q: e.dma_start(out=w_dn[:, q * 11:(q + 1) * 11, :], in_=wdn_v[:, q * 11:(q + 1) * 11, :]),
                     writes=[dnres[q]], dma="f_dn%d" % q)

            cw = arena.alloc([NJ, 3], F32)
            cb = arena.alloc([NJ], F32)
            ln2g = arena.alloc([DM], F32)
            ln2b = arena.alloc([DM], F32)
            halo = arena.alloc([NJ, 2], F32)
            r_c = Res("cF")
            r_halo = [Res() for _ in range(NJ)]
            S.op("sp", lambda e: e.dma_start(out=cw, in_=cw_d[:, :, :]), writes=[r_c], dma="f0")
            S.op("sp", lambda e: e.dma_start(out=cb, in_=cb_d[:, :]), writes=[r_c], dma="f1")
            S.op("sp", lambda e: e.dma_start(out=ln2g, in_=ln2g_d[0:1, :].broadcast_to([128, DM])), writes=[r_c], dma="f2")
            S.op("sp", lambda e: e.dma_start(out=ln2b, in_=ln2b_d[0:1, :].broadcast_to([128, DM])), writes=[r_c], dma="f3")
            S.op("dve", lambda e: e.memset(halo, 0.0), writes=r_halo)
            h1a = [arena.alloc([DM], F32) for _ in range(2)]
            h1ares = [Res(), Res()]
            h1b = [arena.alloc([DM], F32) for _ in range(2)]
            h1bres = [Res(), Res()]
            h1T = [arena.alloc([KC, TT], BF16) for _ in range(2)]
            h1Tres = [[Res() for _ in range(4)] for _ in range(2)]
            aext = [arena.alloc([TT + 2], F32) for _ in range(2)]
            ares = [Res(), Res()]
            cc = [arena.alloc([TT], F32) for _ in range(2)]
            ccres = [Res(), Res()]
            gT = arena.alloc([NJ, TT], BF16)
            gres = [Res() for _ in range(NJ)]
            a_i = [0]
            z_i = [0]
            tp_i = [0]

            def stage_tr(t):
                hT = h1T[t % 2]
                for sub in range(4):
                    s_ = (t * 4 + sub) % 2
                    row0 = t * TT + sub * 128
                    S.op("sp", lambda e, s_=s_, row0=row0: e.dma_start(out=h1a[s_], in_=h1_d[row0:row0 + 128, :]),
                         reads=[h1res[t][sub]], writes=[h1ares[s_]], dma="h1a%d" % s_)
                    b0 = (tp_i[0] % 2) * 2
                    tp_i[0] += 1

                    def tr(e, s_=s_, b0=b0):
                        last = None
                        for k in range(KC):
                            last = e.transpose(bank[b0 + k // 4][:, (k % 4) * 128:(k % 4 + 1) * 128],
                                               h1a[s_][:, k * 128:(k + 1) * 128], ident_f)
                        return last
                    S.op("pe", tr, reads=[h1ares[s_], r_const], writes=[bres[b0], bres[b0 + 1]])
                    for hb in range(2):
                        S.op("act", lambda e, sub=sub, b0=b0, hb=hb, hT=hT: e.copy(
                            out=hT[:, hb * 4:(hb + 1) * 4, sub * 128:(sub + 1) * 128],
                            in_=bank[b0 + hb].rearrange("p (b c) -> p b c", c=128)),
                            reads=[bres[b0 + hb]], writes=[h1Tres[t % 2][sub]])

            def stage_up(t):
                hT = h1T[t % 2]
                hTr = h1Tres[t % 2]
                for j in range(NJ):
                    a_ = a_i[0] % 2
                    a_i[0] += 1
                    pa, pra = next_bank(4, 4)
                    ca = j * 128
                    S.op("pe", mm_acc(pa, [(w_up_a[jchunk(j)][:, k, (j - jch[jchunk(j)][0]) * 128:(j - jch[jchunk(j)][0] + 1) * 128], hT[:, k, :]) for k in range(KC)]),
                         reads=[upres_a[jchunk(j)]] + hTr, writes=[pra])
                    pb, prb = next_bank(4, 4)
                    cbb = D_FF + j * 128
                    S.op("pe", mm_acc(pb, [(w_up_b[jchunk(j)][:, k, (j - jch[jchunk(j)][0]) * 128:(j - jch[jchunk(j)][0] + 1) * 128], hT[:, k, :]) for k in range(KC)]),
                         reads=[upres_b[jchunk(j)]] + hTr, writes=[prb])
                    S.op("act", lambda e, a_=a_, pa=pa: e.copy(out=aext[a_][:, 2:TT + 2], in_=pa),
                         reads=[pra], writes=[ares[a_]])
                    S.op("act", lambda e, a_=a_, j=j, pa=pa: e.activation(
                        out=cc[a_], in_=pa, func=AF.Identity, bias=cb[:, j:j + 1], scale=cw[:, j, 2:3]),
                        reads=[pra, r_c], writes=[ccres[a_]])
                    S.op("act", lambda e, a_=a_, j=j: e.copy(out=aext[a_][:, 0:2], in_=halo[:, j, :]),
                         reads=[r_halo[j]], writes=[ares[a_]])
                    S.op("act", lambda e, a_=a_, j=j: e.copy(out=halo[:, j, :], in_=aext[a_][:, TT:TT + 2]),
                         reads=[ares[a_]], writes=[r_halo[j]])
                    S.op("dve", lambda e, a_=a_, j=j: e.scalar_tensor_tensor(
                        out=cc[a_], in0=aext[a_][:, 1:TT + 1], scalar=cw[:, j, 1:2], in1=cc[a_], op0=ALU.mult, op1=ALU.add),
                        reads=[ares[a_]], writes=[ccres[a_]])
                    S.op("dve", lambda e, a_=a_, j=j: e.scalar_tensor_tensor(
                        out=cc[a_], in0=aext[a_][:, 0:TT], scalar=cw[:, j, 0:1], in1=cc[a_], op0=ALU.mult, op1=ALU.add),
                        reads=[ares[a_]], writes=[ccres[a_]])
                    S.op("act", lambda e, a_=a_: e.activation(out=cc[a_], in_=cc[a_], func=AF.Gelu_apprx_tanh),
                         reads=[ccres[a_]], writes=[ccres[a_]])
                    S.op("dve", lambda e, a_=a_, j=j, pb=pb: e.tensor_tensor(out=gT[:, j, :], in0=pb, in1=cc[a_], op=ALU.mult),
                         reads=[prb, ccres[a_]], writes=[gres[j]])

            def stage_down(t):
                for sub in range(4):
                    ts_ = slice(sub * 128, (sub + 1) * 128)
                    z_ = z_i[0] % 2
                    z_i[0] += 1
                    row0 = t * TT + sub * 128
                    S.op("sp", lambda e, z_=z_, row0=row0: e.dma_start(out=h1b[z_], in_=h1_d[row0:row0 + 128, :]),
                         reads=[h1res[t][sub]], writes=[h1bres[z_]], dma="h1b%d" % z_)
                    for half in range(2):
                        hs_ = slice(half * 512, (half + 1) * 512)
                        pb, pr = next_bank(4, 4)
                        S.op("pe", mm_acc(pb, [(gT[:, j, ts_], w_dn[:, j, hs_]) for j in range(NJ)]),
                             reads=gres + dnres, writes=[pr])
                        S.op("dve", lambda e, z_=z_, hs_=hs_, pb=pb: e.scalar_tensor_tensor(
                            out=h1b[z_][:, hs_], in0=h1b[z_][:, hs_], scalar=ALPHA, in1=pb, op0=ALU.mult, op1=ALU.add),
                            reads=[pr], writes=[h1bres[z_]])
                    ln_norm(h1b[z_], h1bres[z_], DM)
                    ln_affine(h1b[z_], h1bres[z_], ln2g, ln2b, r_c, h1b[z_], h1bres[z_], "pool")
                    S.op("pool", lambda e, z_=z_, row0=row0: e.dma_start(out=out_d[row0:row0 + 128, :], in_=h1b[z_]),
                         reads=[h1bres[z_]], writes=[outres], dma="ow%d" % z_)

            stage_tr(0)
            for t in range(NT):
                stage_up(t)
                if t + 1 < NT:
                    stage_tr(t + 1)
                stage_down(t)

        print('arena after A', arena.peak)
        S.barrier()
        arena.top = mark_T
        arena.peak = 0
        phase_T()
        print('arena T', arena.peak)
        if stop_after == "T":
            S.wait_all("sp", [r for rr in h1res for r in rr])
        else:
            S.barrier()
            arena.top = f_base
            phase_F()
            S.wait_all("sp", [outres])
        with nc.Block() as block:
            S.emit_all(block)
        print("SBUF arena peak units", arena.peak, "of", arena.n)
    return nc


def _t5_bucket(dist):
    max_exact = 16
    n = np.maximum(dist, 1).astype(np.float32)
    val = (np.log(n / np.float32(max_exact)) / np.float32(math.log(2048 / max_exact))
           * np.float32(32 - max_exact))
    large = max_exact + val.astype(np.int32)
    large = np.minimum(large, 31)
    return np.where(dist < max_exact, dist, large)


def _bias_tiles(rel_bias):
    kj = np.arange(128)[:, None]
    qc = np.arange(256)[None, :]
    steps = qc - kj
    valid = (steps >= 0) & (steps <= 128)
    out = np.empty((128, 12, 256), np.float32)
    for g, d in enumerate(DIL):
        bucket = _t5_bucket(np.maximum(steps, 0) * d)
        for h in range(4):
            col = g * 4 + h
            out[:, col, :] = np.where(valid, rel_bias[bucket, col], np.float32(-1e30))
    return out


def _prep_shared(inp):
    f = lambda a: np.ascontiguousarray(a, dtype=np.float32)
    sh = {}
    sh["biasA"] = _bias_tiles(f(inp["rel_bias"]))
    sh["ident"] = np.eye(128, dtype=np.float32)
    sh["w_in"] = f(inp["w_in"][0])
    wq = np.empty((DM, 6, 384), np.float32)
    for it in range(6):
        p, g = divmod(it, 3)
        for i, base in enumerate((0, 768, 1536)):
            c0 = base + g * 256 + p * 128
            wq[:, it, i * 128:(i + 1) * 128] = inp["w_in"][0][:, c0:c0 + 128]
    sh["w_qkv_r"] = wq
    sh["w_mem_kv"] = f(inp["w_mem_kv"][0])
    sh["sg_ln_g"] = f(inp["sg_ln_g"][0][None, :])
    sh["sg_ln_bT"] = f(inp["sg_ln_b"][0].reshape(4, 128).T)
    sh["wspT"] = f(np.transpose(inp["w_spatial"][0], (2, 0, 1)))
    sh["trilT"] = f(np.triu(np.ones((128, 128), np.float32)))
    bsp = inp["b_spatial"][0]
    t = bsp.reshape(4, 2, 128)
    t = np.transpose(t, (1, 0, 2))
    sh["bspT"] = f(np.broadcast_to(t[:, None, :, :], (2, 64, 4, 128)).reshape(128, 4, 128))
    sh["w_proj_a"] = f(inp["w_proj_a"][0])
    sh["w_proj_b"] = f(inp["w_proj_b"][0])
    sh["w_proj_c"] = f(inp["w_proj_c"][0])
    sh["w_gate_r"] = f(np.transpose(inp["w_gate"][0].reshape(DM, 3, 8, 128), (0, 2, 1, 3)).reshape(DM, 8, 384))
    sh["b_gateT"] = f(inp["b_gate"][0].reshape(24, 128).T)
    sh["w_out"] = f(inp["w_out"][0])
    sh["ln1_g"] = f(inp["ln1_g"][0][None, :])
    sh["ln1_b"] = f(inp["ln1_b"][0][None, :])
    sh["w_ffn_up"] = f(inp["w_ffn_up"][0])
    sh["conv_wT"] = f(np.transpose(inp["conv_w"][0].T.reshape(NJ, 128, 3), (1, 0, 2)))
    sh["conv_bT"] = f(inp["conv_b"][0].reshape(NJ, 128).T)
    sh["w_ffn_down"] = f(inp["w_ffn_down"][0])
    sh["ln2_g"] = f(inp["ln2_g"][0][None, :])
    sh["ln2_b"] = f(inp["ln2_b"][0][None, :])
    return sh


def _in_maps(inp, cores):
    sh = _prep_shared(inp)
    maps = []
    for b in cores:
        m = dict(sh)
        xb = np.ascontiguousarray(inp["x"][b], dtype=np.float32)
        m["x"] = xb
        m["xT"] = np.ascontiguousarray(xb.T)
        m["memT"] = np.ascontiguousarray(np.asarray(inp["mem"][b], dtype=np.float32).T)
        maps.append(m)
    return maps


def kernel(**inputs):
    nc = build_program()
    maps = _in_maps(inputs, list(range(N_CORES)))
    res = run_bass_kernel_spmd(nc, maps, core_ids=list(range(N_CORES)))
    out = np.stack([np.asarray(r["out"], dtype=np.float32) for r in res.results], axis=0)
    return out
```

```python
import math
from contextlib import ExitStack

import numpy as np
import concourse.bass as bass
import concourse.mybir as mybir
from concourse.bass_utils import run_bass_kernel_spmd

F32 = mybir.dt.float32
BF16 = mybir.dt.bfloat16
AF = mybir.ActivationFunctionType
ALU = mybir.AluOpType

SEQ = 4096
DM = 1024
KC = 8
TT = 512
NT = SEQ // TT
DIL = (1, 4, 16)
D_FF = 2816
NJ = D_FF // 128
ALPHA = float(2.0 ** 0.25)
LN_EPS = 1e-5
N_CORES = 8


class Res:
    __slots__ = ("name", "last_w", "readers")

    def __init__(self, name=""):
        self.name = name
        self.last_w = None
        self.readers = []


class Sched:
    def __init__(self, nc, stack):
        self.nc = nc
        self.stack = stack
        self.engs = ("pe", "act", "dve", "pool", "sp")
        self.sem = {}
        self.cnt = {}
        self.semobj = {}
        for e in self.engs:
            self.sem[e] = stack.enter_context(nc.semaphore("c_" + e))
            self.cnt[e] = 0
            self.semobj[id(self.sem[e])] = self.sem[e]
        self.seen = {e: {} for e in self.engs}
        self.dma_sems = {}
        self.dma_cnt = {}
        self.prog = {e: [] for e in self.engs}

    def dma_sem(self, key):
        if key not in self.dma_sems:
            s = self.stack.enter_context(self.nc.semaphore("d_" + str(key)))
            self.dma_sems[key] = s
            self.dma_cnt[key] = 0
            self.semobj[id(s)] = s
        return self.dma_sems[key]

    def _waits(self, e, reads, writes):
        need = {}

        def add(tok):
            if tok is None:
                return
            sid, v, owner = tok
            if owner == "pe" and e == "pe":
                return
            if need.get(sid, 0) < v:
                need[sid] = v

        for r in reads:
            add(r.last_w)
        for w in writes:
            add(w.last_w)
            for t in w.readers:
                add(t)
        out = []
        for sid, v in need.items():
            if self.seen[e].get(sid, 0) < v:
                self.seen[e][sid] = v
                out.append((self.semobj[sid], v))
        return out

    def op(self, e, emit, reads=(), writes=(), dma=None, attach=True):
        waits = self._waits(e, reads, writes)
        if dma is not None:
            s = self.dma_sem(dma)
            self.dma_cnt[dma] += 16
            inc = (s, 16)
            tok = (id(s), self.dma_cnt[dma], "dma")
        else:
            self.cnt[e] += 1
            inc = (self.sem[e], 1)
            tok = (id(self.sem[e]), self.cnt[e], e)
        can_attach = attach and e in ("act", "dve", "pool") and dma is None
        self.prog[e].append((waits, emit, inc, can_attach))
        for r in reads:
            r.readers.append(tok)
        for w in writes:
            w.last_w = tok
            w.readers = []
        return tok

    def wait_all(self, e, resources):
        waits = self._waits(e, resources, ())
        self.prog[e].append((waits, None, None, False))

    def barrier(self):
        allsems = [(self.sem[e], self.cnt[e]) for e in self.engs]
        allsems += [(self.dma_sems[k], self.dma_cnt[k]) for k in self.dma_sems]
        for e in self.engs:
            waits = []
            for s, v in allsems:
                if v > 0 and self.seen[e].get(id(s), 0) < v:
                    self.seen[e][id(s)] = v
                    waits.append((s, v))
            self.prog[e].append((waits, None, None, False))

    def _replay(self, e, eng):
        for waits, emit, inc, can_attach in self.prog[e]:
            att = None
            if can_attach and waits:
                att = waits[-1]
                waits = waits[:-1]
            for s, v in waits:
                eng.wait_ge(s, v)
            if emit is None:
                continue
            ins = emit(eng)
            if isinstance(ins, (list, tuple)):
                first, last = ins[0], ins[-1]
            else:
                first = last = ins
            if att is not None:
                first._wait_ge(att[0], att[1])
            last.then_inc(inc[0], inc[1])

    def emit_all(self, block):
        S = self

        @block.tensor
        def _(eng):
            S._replay("pe", eng)

        @block.scalar
        def _(eng):
            S._replay("act", eng)

        @block.vector
        def _(eng):
            S._replay("dve", eng)

        @block.gpsimd
        def _(eng):
            S._replay("pool", eng)

        @block.sync
        def _(eng):
            S._replay("sp", eng)


class Arena:
    def __init__(self, nc, stack, nunits):
        self.h = stack.enter_context(nc.sbuf_tensor("arena", [128, nunits], BF16))
        self.n = nunits
        self.top = 0
        self.peak = 0

    def alloc(self, free, dt):
        n = 1
        for f in free:
            n *= int(f)
        units = n * (2 if dt == F32 else 1)
        off = self.top
        self.last_off = off
        self.top += (units + 15) // 16 * 16
        self.peak = max(self.peak, self.top)
        assert self.top <= self.n, ("SBUF arena overflow", self.top, self.n)
        v = self.h[:, off:off + units]
        if dt == F32:
            v = v.bitcast(F32)
        if len(free) > 1:
            names = "abcdefg"[:len(free)]
            pat = "p (" + " ".join(names) + ") -> p " + " ".join(names)
            v = v.rearrange(pat, **{k: int(s) for k, s in zip(names, free)})
        return v


def build_program(stop_after=None):
    nc = bass.Bass("TRN2", target_bir_lowering=False)

    def din(name, shape):
        return nc.dram_tensor(name, list(shape), F32, kind="ExternalInput").ap()

    x_d = din("x", [SEQ, DM])
    xT_d = din("xT", [DM, SEQ])
    memT_d = din("memT", [DM, 256])
    biasA_d = din("biasA", [128, 12, 256])
    ident_d = din("ident", [128, 128])
    w_in_d = din("w_in", [DM, 3584])
    wqkv_d = din("w_qkv_r", [DM, 6, 384])
    w_kv_d = din("w_mem_kv", [DM, 512])
    sgg_d = din("sg_ln_g", [1, 512])
    sgbT_d = din("sg_ln_bT", [128, 4])
    wspT_d = din("wspT", [128, 8, 128])
    trilT_d = din("trilT", [128, 128])
    bsp_d = din("bspT", [128, 4, 128])
    wpa_d = din("w_proj_a", [256, DM])
    wpb_d = din("w_proj_b", [512, DM])
    wpc_d = din("w_proj_c", [256, DM])
    wg_d = din("w_gate_r", [DM, 8, 384])
    bg_d = din("b_gateT", [128, 24])
    wo_d = din("w_out", [DM, DM])
    ln1g_d = din("ln1_g", [1, DM])
    ln1b_d = din("ln1_b", [1, DM])
    wup_d = din("w_ffn_up", [DM, 2 * D_FF])
    cw_d = din("conv_wT", [128, NJ, 3])
    cb_d = din("conv_bT", [128, NJ])
    wdn_d = din("w_ffn_down", [D_FF, DM])
    ln2g_d = din("ln2_g", [1, DM])
    ln2b_d = din("ln2_b", [1, DM])
    out_d = nc.dram_tensor("out", [SEQ, DM], F32, kind="ExternalOutput").ap()
    h1_d = nc.dram_tensor("h1_scratch", [SEQ, DM], F32, kind="ExternalOutput").ap()
    dbg_d = None
    if stop_after == "A":
        dbg_d = nc.dram_tensor("dbg", [128, 2, SEQ], F32, kind="ExternalOutput").ap()

    with ExitStack() as stack:
        arena = Arena(nc, stack, 106368)
        ps_h = stack.enter_context(nc.psum_tensor("ps", [128, 8, 512], F32))
        ps = ps_h[:]
        bank = [ps[:, i, :] for i in range(8)]
        bres = [Res("bank%d" % i) for i in range(8)]
        S = Sched(nc, stack)

        ident_f = arena.alloc([128], F32)
        ident_b = arena.alloc([128], BF16)
        eps_t = arena.alloc([1], F32)
        mhalf_t = arena.alloc([1], F32)
        neg1_t = arena.alloc([1], F32)
        r_const = Res("const")
        S.op("sp", lambda e: e.dma_start(out=ident_f, in_=ident_d[:, :]), writes=[r_const], dma="c0")
        S.op("pool", lambda e: e.dma_start(out=ident_b, in_=ident_d[:, :]), writes=[r_const], dma="c1")
        S.op("dve", lambda e: e.memset(eps_t, LN_EPS), writes=[r_const])
        S.op("dve", lambda e: e.memset(mhalf_t, -0.5), writes=[r_const])
        S.op("dve", lambda e: e.memset(neg1_t, -1.0), writes=[r_const])
        ln_slots = []
        for i in range(4):
            ln_slots.append((arena.alloc([2, 6], F32), arena.alloc([2], F32), arena.alloc([1], F32), arena.alloc([1], F32), Res("ln%d" % i)))
        const_mark = arena.top
        oaT = arena.alloc([2, SEQ], BF16)
        oares = [Res("oa0"), Res("oa1")]
        kcT = arena.alloc([4, 256], BF16)
        vca = arena.alloc([2, 2, 3, 64], BF16)
        wspT = arena.alloc([8, 128], BF16)
        r_kv = Res("kv")
        r_wsp = Res("wsp")
        r_min = Res("min")
        w_bc = arena.alloc([KC, 1280], BF16)
        r_wbc = Res("wbc")
        base_mark = arena.top
        mark_T = base_mark

        xT_v = xT_d.rearrange("(kc p) t -> p kc t", p=128)
        w_in_v = w_in_d.rearrange("(kc p) n -> p kc n", p=128)

        def phase_A():
            xTb = arena.alloc([KC, SEQ], BF16)
            xres = [Res("xT%d" % t) for t in range(NT)]
            wslab = [arena.alloc([KC, 3, 128], BF16) for _ in range(2)]
            wsl = {n: [wslab[s_][:, :, i, :] for s_ in range(2)] for i, n in enumerate("qkv")}
            wslres = [Res("wslab0"), Res("wslab1")]
            wres = {n: wslres for n in "qkv"}
            wqkv_v = wqkv_d.rearrange("(k p) i n -> p k i n", p=128)

            def load_w(it):
                slot = it % 2
                S.op("pool", lambda e, it=it, slot=slot: e.dma_start(
                    out=wslab[slot].rearrange("p k i n -> p k (i n)"), in_=wqkv_v[:, :, it, :]),
                    writes=[wslres[slot]], dma="wslab%d" % slot)

            load_w(0)
            for t in range(NT):
                S.op("pool", lambda e, t=t: e.dma_start(out=xTb[:, :, t * TT:(t + 1) * TT],
                                                        in_=xT_v[:, :, t * TT:(t + 1) * TT]),
                     writes=[xres[t]], dma="xT%d" % t)
            bst = arena.alloc([12, 256], F32)
            r_b = Res("bias")
            S.op("sp", lambda e: e.dma_start(out=bst, in_=biasA_d[:, :, :]), writes=[r_b], dma="b0")
            sbt = [arena.alloc([256], F32) for _ in range(4)]
            sbres = [Res() for _ in range(4)]
            S.op("pool", lambda e: e.dma_start(out=w_bc, in_=w_in_v[:, :, 2304:3584]), writes=[r_wbc], dma="t0")
            m_state = [False]

            qT = arena.alloc([SEQ], BF16)
            kTz = [arena.alloc([SEQ], BF16) for _ in range(2)]
            r_kz = Res("kz")
            S.op("pool", lambda e: e.memset(kTz[0][64:128, :], 0.0), writes=[r_kz])
            S.op("pool", lambda e: e.memset(kTz[1][0:64, :], 0.0), writes=[r_kz])
            qres = [Res() for _ in range(NT)]
            kres = [Res() for _ in range(NT)]
            V = arena.alloc([32, 3, 64], BF16)
            vres = [Res() for _ in range(8)]
            r_ones = Res("ones")
            S.op("pool", lambda e: e.memset(V[:, :, 1, :], 1.0), writes=[r_ones])
            acc = []
            acc_off = []
            for _ in range(2):
                acc.append(arena.alloc([SEQ], F32))
                acc_off.append(arena.last_off)
            accres = [Res("acc0"), Res("acc1")]
            o0, o1 = acc_off
            m_memTb = arena.h[:, o0:o0 + 2048].rearrange("p (k m) -> p k m", k=KC)
            m_wkv = arena.h[:, o0 + 2048:o0 + 6144].rearrange("p (k n) -> p k n", k=KC)
            m_wsp_f = arena.h[:, o1:o1 + 2048].bitcast(F32).rearrange("p (g q) -> p g q", g=8)
            m_tril_f = arena.h[:, o1 + 2048:o1 + 2304].bitcast(F32)
            m_args = (m_memTb, m_wkv, m_wsp_f, m_tril_f, accres[0], accres[1])
            recs = [arena.alloc([256], F32) for _ in range(1)]
            recress = [Res("rec0"), Res("rec1")]
            PT = [arena.alloc([256], BF16) for _ in range(6)]
            ptres = [Res() for _ in range(6)]
            st_slots = [(bank[i][:, 0:256], bres[i]) for i in range(4)]
            obanks = [(bank[4], bres[4]), (bank[5], bres[5])]
            pbanks = [(bank[6 + i], bres[6 + i]) for i in range(2)]
            pb_i = [0]

            def next_pb():
                b = pbanks[pb_i[0] % 2]
                pb_i[0] += 1
                return b

            S.op("pool", lambda e: e.dma_start(out=w_bc, in_=w_in_v[:, :, 2304:3584]), writes=[r_wbc], dma="t0")
            m_state = [False]
            pending = []

            def finalize_ops(p, hh):
                ops = []
                nr = slice(hh * 64, hh * 64 + 64)
                dr = slice((1 - hh) * 64, (1 - hh) * 64 + 64)
                for c in range(16):
                    cs = slice(c * 256, (c + 1) * 256)

                    def f(nr=nr, dr=dr, cs=cs, hh=hh, p=p, c=c):
                        rec, recres = recs[0], recress[0]
                        S.op("dve", lambda e: e.reciprocal(out=rec[nr, :], in_=acc[hh][dr, cs]),
                             reads=[accres[hh]], writes=[recres])
                        S.op("dve", lambda e: e.tensor_tensor(
                            out=oaT[nr, p, cs], in0=acc[hh][nr, cs], in1=rec[nr, :], op=ALU.mult),
                            reads=[accres[hh], recres], writes=[oares[p]])
                    ops.append(f)
                return ops

            st_i = [0]
            ob_i = [0]
            for it in range(6):
                p, g = divmod(it, 3)
                slot = it % 2
                d = DIL[g]
                nbr = 32 // d
                if it == 0:
                    phase_M_loads(*m_args)
                if it + 1 < 6:
                    load_w(it + 1)
                for t in range(NT):
                    for n in ("q", "k"):
                        pb, pr = next_pb()

                        def mm(e, n=n, t=t, pb=pb, slot=slot):
                            last = None
                            for kc in range(KC):
                                last = e.matmul(pb, wsl[n][slot][:, kc, :], xTb[:, kc, t * TT:(t + 1) * TT],
                                                start=(kc == 0), stop=(kc == KC - 1))
                            return last
                        S.op("pe", mm, reads=[wres[n][slot], xres[t]], writes=[pr])
                        w_ = TT // d
                        if n == "q":
                            S.op("act", lambda e, t=t, pb=pb, d=d, w_=w_: e.mul(
                                qT.rearrange("p (r i) -> p r i", r=d)[:, :, t * w_:(t + 1) * w_],
                                pb.rearrange("p (i r) -> p r i", r=d), 0.125),
                                reads=[pr], writes=[qres[t]])
                        else:
                            def kcopy(e, t=t, pb=pb, d=d, w_=w_):
                                src = pb.rearrange("p (i r) -> p r i", r=d)
                                e.copy(out=kTz[0].rearrange("p (r i) -> p r i", r=d)[0:64, :, t * w_:(t + 1) * w_], in_=src[0:64])
                                return e.copy(out=kTz[1].rearrange("p (r i) -> p r i", r=d)[64:128, :, t * w_:(t + 1) * w_], in_=src[64:128])
                            S.op("act", kcopy, reads=[pr], writes=[kres[t]], attach=False)
                            if pending:
                                pending.pop(0)()
                xv = xTb.rearrange("p k (i d) -> p k d i", d=d)
                for bb in range(8):
                    pb, pr = next_pb()

                    def mmv(e, bb=bb, pb=pb, slot=slot, xv=xv, nbr=nbr):
                        last = None
                        for j in range(4):
                            blk = bb * 4 + j
                            r, n = divmod(blk, nbr)
                            for kc in range(KC):
                                last = e.matmul(pb[:, j * 128:(j + 1) * 128],
                                                xv[:, kc, r, n * 128:(n + 1) * 128],
                                                wsl["v"][slot][:, kc, :],
                                                start=(kc == 0), stop=(kc == KC - 1))
                        return last
                    need = set()
                    for j in range(4):
                        r_, n_ = divmod(bb * 4 + j, nbr)
                        lo_t = n_ * 128 * d + r_
                        hi_t = (n_ * 128 + 127) * d + r_
                        need.update(range(lo_t // TT, hi_t // TT + 1))
                    S.op("pe", mmv, reads=[wres["v"][slot]] + [xres[i] for i in sorted(need)], writes=[pr])
                    S.op("act", lambda e, bb=bb, pb=pb: e.copy(
                        out=V[:, bb * 4:(bb + 1) * 4, 0:3:2, :],
                        in_=pb.rearrange("p (a b c) -> p a b c", a=4, b=2, c=64)),
                        reads=[pr], writes=[vres[bb]])
                if it == 0:
                    phase_M_compute(*m_args, pbanks)
                qv = qT.rearrange("p (r i) -> p r i", r=d)
                kvz = [kTz[h_].rearrange("p (r i) -> p r i", r=d) for h_ in range(2)]
                for hh in range(2):
                    hg = g * 4 + p * 2 + hh
                    lo_, hi_ = hh * 64, hh * 64 + 64
                    accv = acc[hh].rearrange("p (i d) -> p d i", d=d)
                    steps = [(r, j) for r in range(d) for j in range(nbr)]
                    info = {}

                    def emitS(i, steps=steps, qv=qv, kv=kvz[hh], hg=hg, lo_=lo_, hi_=hi_, nbr=nbr, info=info):
                        r, j = steps[i]
                        nq = 256 if j + 1 < nbr else 128
                        sap, sres = st_slots[st_i[0] % 4]
                        st_i[0] += 1
                        pt_idx = i % 6

                        def mm(e):
                            return e.matmul(sap[:, :nq], kv[:, r, j * 128:(j + 1) * 128],
                                            qv[:, r, j * 128:j * 128 + nq], start=True, stop=True)
                        S.op("pe", mm, reads=qres + kres + [r_kz], writes=[sres])
                        sb_idx = i % 4
                        sb = sbt[sb_idx]
                        S.op("dve", lambda e: e.tensor_tensor(out=sb[:, :nq], in0=sap[:, :nq], in1=bst[:, hg, :nq], op=ALU.add),
                             reads=[sres, r_b], writes=[sbres[sb_idx]])
                        S.op("act", lambda e: e.activation(out=PT[pt_idx][:, :nq], in_=sb[:, :nq], func=AF.Exp),
                             reads=[sbres[sb_idx]], writes=[ptres[pt_idx]])
                        info[i] = pt_idx

                    def emitPV(i, steps=steps, hh=hh, nbr=nbr, accv=accv, info=info, g=g, d=d):
                        r, j = steps[i]
                        pos = i % 4
                        oap, ores = obanks[ob_i[0] % 2]
                        blk = r * nbr + j
                        sl = slice(0, 2) if hh == 0 else slice(1, 3)

                        def mm(e):
                            if j > 0:
                                e.matmul(oap[:, pos * 128:(pos + 1) * 128],
                                         V[:, blk - 1, sl, :].rearrange("p a b -> p (a b)"),
                                         PT[info[i - 1]][:, 128:256], start=True, stop=False)
                            return e.matmul(oap[:, pos * 128:(pos + 1) * 128],
                                            V[:, blk, sl, :].rearrange("p a b -> p (a b)"),
                                            PT[info[i]][:, 0:128], start=(j == 0), stop=True)
                        rd = [ptres[info[i]], vres[blk // 4], r_ones]
                        if j > 0:
                            rd += [ptres[info[i - 1]], vres[(blk - 1) // 4]]
                        S.op("pe", mm, reads=rd, writes=[ores])
                        if pos == 3:
                            m = i // 4
                            if nbr >= 4:
                                r0, n0 = divmod(4 * m, nbr)
                                dst = accv[:, r0, n0 * 128:(n0 + 4) * 128]
                                src = oap
                            else:
                                r0 = (4 * m) // nbr
                                dst = accv[:, r0:r0 + 2, :]
                                src = oap.rearrange("p (a b) -> p a b", a=2)
                            if g == 0:
                                S.op("dve", lambda e: e.tensor_copy(out=dst, in_=src),
                                     reads=[ores], writes=[accres[hh]])
                            else:
                                S.op("dve", lambda e: e.tensor_tensor(out=dst, in0=src, in1=dst, op=ALU.add),
                                     reads=[ores], writes=[accres[hh]])
                            ob_i[0] += 1
                            for _ in range(2):
                                if pending:
                                    pending.pop(0)()

                    emitS(0)
                    emitS(1)
                    emitS(2)
                    for i in range(len(steps)):
                        if i + 3 < len(steps):
                            emitS(i + 3)
                        emitPV(i)
                    if g == 2:
                        fo = finalize_ops(p, hh)
                        pending.extend(fo)
                        if hh == 1 and it == 5:
                            while pending:
                                pending.pop(0)()

        bk_i = [0]

        def next_bank(lo=0, n=8):
            i = lo + bk_i[0] % n
            bk_i[0] += 1
            return bank[i], bres[i]

        def mm_acc(pb, pairs):
            def f(e):
                last = None
                for i, (l, r) in enumerate(pairs):
                    last = e.matmul(pb, l, r, start=(i == 0), stop=(i == len(pairs) - 1))
                return last
            return f

        ln_i = [0]

        def ln_norm(zt, zres, width):
            st6, mv, rstd, nmr, r_ln = ln_slots[ln_i[0] % 4]
            ln_i[0] += 1
            nch = width // 512

            def stats(e):
                last = None
                for c in range(nch):
                    last = e.bn_stats(out=st6[:, c, :], in_=zt[:, c * 512:(c + 1) * 512])
                return last
            S.op("dve", stats, reads=[zres], writes=[r_ln], attach=False)
            S.op("dve", lambda e: e.bn_aggr(out=mv, in_=st6[:, 0:nch, :].rearrange("p a b -> p (a b)")),
                 reads=[r_ln], writes=[r_ln])
            S.op("pool", lambda e: e.tensor_tensor(out=rstd, in0=mv[:, 1:2], in1=eps_t, op=ALU.add),
                 reads=[r_ln, r_const], writes=[r_ln])
            S.op("pool", lambda e: e.tensor_tensor(out=rstd, in0=rstd, in1=mhalf_t, op=ALU.pow),
                 reads=[r_ln, r_const], writes=[r_ln])
            S.op("pool", lambda e: e.tensor_tensor(out=nmr, in0=mv[:, 0:1], in1=rstd, op=ALU.mult),
                 reads=[r_ln], writes=[r_ln])
            S.op("pool", lambda e: e.tensor_tensor(out=nmr, in0=nmr, in1=neg1_t, op=ALU.mult),
                 reads=[r_ln, r_const], writes=[r_ln])
            S.op("act", lambda e: e.activation(out=zt, in_=zt, func=AF.Identity, bias=nmr[:, 0:1], scale=rstd[:, 0:1]),
                 reads=[r_ln, zres], writes=[zres])

        def ln_affine(zt, zres, gt, bt, r_gb, out_t, out_res, aff):
            wr = [out_res] if out_res is zres else [out_res, zres]
            if bt is None:
                S.op(aff, lambda e: e.tensor_tensor(out=out_t, in0=zt, in1=gt, op=ALU.mult),
                     reads=[zres, r_gb], writes=wr)
                return
            S.op(aff, lambda e: e.tensor_tensor(out=zt, in0=zt, in1=gt, op=ALU.mult),
                 reads=[zres, r_gb], writes=[zres])
            S.op(aff, lambda e: e.tensor_tensor(out=out_t, in0=zt, in1=bt, op=ALU.add),
                 reads=[zres, r_gb], writes=wr)


        def phase_M_loads(memTb, wkv, wsp_f, tril_f, g0, g1):
            S.op("pool", lambda e: e.dma_start(out=memTb, in_=memT_d.rearrange("(k p) m -> p k m", p=128)),
                 writes=[r_min, g0], dma="m0")
            S.op("pool", lambda e: e.dma_start(out=wkv, in_=w_kv_d.rearrange("(k p) n -> p k n", p=128)),
                 writes=[r_min, g0], dma="m1")
            S.op("sp", lambda e: e.dma_start(out=wsp_f, in_=wspT_d[:, :, :]), writes=[r_wsp, g1], dma="m2")
            S.op("sp", lambda e: e.dma_start(out=tril_f, in_=trilT_d[:, :]), writes=[r_wsp, g1], dma="m3")

        def phase_M_compute(memTb, wkv, wsp_f, tril_f, g0, g1, banks):
            S.op("dve", lambda e: e.tensor_tensor(out=wspT, in0=wsp_f,
                                                  in1=tril_f.unsqueeze(1).broadcast_to([128, 8, 128]), op=ALU.mult),
                 reads=[r_wsp, g1], writes=[r_wsp])
            S.op("pool", lambda e: e.memset(vca[:, :, :, 1, :], 1.0), writes=[r_kv])
            S.op("pool", lambda e: e.memset(kcT, 0.0), writes=[r_kv])
            bi = 0
            for c in range(2):
                pb, pr = banks[bi % len(banks)]
                bi += 1
                S.op("pe", mm_acc(pb[:, 0:256], [(wkv[:, k, c * 128:(c + 1) * 128], memTb[:, k, :]) for k in range(KC)]),
                     reads=[r_min, g0], writes=[pr])

                def kccopy(e, c=c, pb=pb):
                    e.tensor_copy(out=kcT[0:64, 2 * c, :], in_=pb[0:64, 0:256])
                    return e.tensor_copy(out=kcT[64:128, 2 * c + 1, :], in_=pb[64:128, 0:256])
                S.op("dve", kccopy, reads=[pr], writes=[r_kv], attach=False)
            for mc in range(2):
                pb, pr = banks[bi % len(banks)]
                bi += 1
                S.op("pe", mm_acc(pb[:, 0:256], [(memTb[:, k, mc * 128:(mc + 1) * 128], wkv[:, k, 256:512]) for k in range(KC)]),
                     reads=[r_min, g0], writes=[pr])
                S.op("dve", lambda e, mc=mc, pb=pb: e.tensor_copy(
                    out=vca[:, mc, :, 0:3:2, :],
                    in_=pb[:, 0:256].rearrange("p (a b c) -> p a b c", a=2, b=2, c=64)),
                    reads=[pr], writes=[r_kv])

        phase_A()
        if stop_after == "A":
            dbg_t = arena.alloc([SEQ], F32)
            r_dbg = Res("dbg")
            r_o = Res("o")
            for c in range(2):
                S.op("dve", lambda e, c=c: e.tensor_copy(out=dbg_t, in_=oaT[:, c, :]), reads=oares, writes=[r_dbg])
                S.op("sp", lambda e, c=c: e.dma_start(out=dbg_d[:, c, :], in_=dbg_t), reads=[r_dbg], writes=[r_o], dma="o")
            S.wait_all("sp", [r_o])
            with nc.Block() as block:
                S.emit_all(block)
            return nc

        h1res = [[Res() for _ in range(4)] for _ in range(NT)]

        jch = [(0, 6), (6, 12), (12, 18), (18, 22)]
        wup_v = wup_d.rearrange("(k p) n -> p k n", p=128)
        _save_top = arena.top
        arena.top = const_mark
        w_up_a, w_up_b = [], []
        for (j0, j1) in jch:
            w_up_a.append(arena.alloc([KC, (j1 - j0) * 128], BF16))
            w_up_b.append(arena.alloc([KC, (j1 - j0) * 128], BF16))
        f_base = arena.top
        arena.top = _save_top
        upres_a = [Res() for _ in jch]
        upres_b = [Res() for _ in jch]

        def jchunk(j):
            for ci, (j0, j1) in enumerate(jch):
                if j0 <= j < j1:
                    return ci

        def load_up_chunk(ci, guards=()):
            j0, j1 = jch[ci]
            for part, tiles, rr in ((0, w_up_a, upres_a), (1, w_up_b, upres_b)):
                c0, c1 = part * D_FF + j0 * 128, part * D_FF + j1 * 128
                S.op("pool", lambda e, c0=c0, c1=c1, t_=tiles[ci]: e.dma_start(out=t_, in_=wup_v[:, :, c0:c1]),
                     writes=[rr[ci]] + list(guards), dma="f_up%d_%d" % (part, ci))

        def phase_T():
            w_pa = arena.alloc([2, DM], BF16)
            w_pb = arena.alloc([4, DM], BF16)
            w_pc = arena.alloc([2, DM], BF16)
            w_o = arena.alloc([KC, DM], BF16)
            r_wp, r_wo, r_c = Res(), Res(), Res("cT")
            ln1g = arena.alloc([DM], F32)
            ln1b = arena.alloc([DM], F32)
            sgg = arena.alloc([512], F32)
            sgbT = arena.alloc([4], F32)
            bsp = arena.alloc([512], F32)
            bg = arena.alloc([24], F32)
            S.op("sp", lambda e: e.dma_start(out=ln1g, in_=ln1g_d[0:1, :].broadcast_to([128, DM])), writes=[r_c], dma="t1")
            S.op("sp", lambda e: e.dma_start(out=ln1b, in_=ln1b_d[0:1, :].broadcast_to([128, DM])), writes=[r_c], dma="t2")
            S.op("sp", lambda e: e.dma_start(out=sgg, in_=sgg_d[0:1, :].broadcast_to([128, 512])), writes=[r_c], dma="t3")
            S.op("sp", lambda e: e.dma_start(out=sgbT, in_=sgbT_d[:, :]), writes=[r_c], dma="t4")
            S.op("sp", lambda e: e.dma_start(out=bsp, in_=bsp_d.rearrange("p a b -> p (a b)")), writes=[r_c], dma="t5")
            S.op("sp", lambda e: e.dma_start(out=bg, in_=bg_d[:, :]), writes=[r_c], dma="t6")

            xt = [arena.alloc([KC, TT], BF16) for _ in range(2)]
            xtres = [Res(), Res()]
            qcT = arena.alloc([2, TT], BF16)
            qcres = [Res(), Res()]
            ptC = [arena.alloc([TT], BF16) for _ in range(8)]
            ptCres = [Res() for _ in range(8)]
            ocT = arena.alloc([2, TT], BF16)
            ocres = [Res(), Res()]
            recC = arena.alloc([TT], F32)
            r_recC = Res()
            uT = arena.alloc([4, TT], F32)
            ures = [Res() for _ in range(4)]
            vgf = [arena.alloc([512], F32) for _ in range(4)]
            vgfres = [Res() for _ in range(4)]
            vgn = [arena.alloc([512], BF16) for _ in range(4)]
            vgnres = [Res() for _ in range(4)]
            tmpB = arena.alloc([512], F32)
            r_tmpB = Res()
            obT = arena.alloc([4, TT], BF16)
            r_obT = Res()
            wgs = [arena.alloc([KC, 3, 128], BF16) for _ in range(3)]
            wgres = [Res(), Res(), Res()]
            gate = [arena.alloc([TT], F32) for _ in range(6)]
            gres = [Res() for _ in range(6)]
            mt = [arena.alloc([TT], F32) for _ in range(3)]
            mres = [Res() for _ in range(3)]
            mergedT = arena.alloc([KC, TT], BF16)
            mgres = [Res() for _ in range(KC)]
            zt = [arena.alloc([DM], F32) for _ in range(4)]
            zres = [Res() for _ in range(4)]

            def load_wp():
                S.op("pool", lambda e: e.dma_start(out=w_pa, in_=wpa_d.rearrange("(c p) n -> p c n", p=128)), writes=[r_wp], dma="t7")
                S.op("pool", lambda e: e.dma_start(out=w_pb, in_=wpb_d.rearrange("(c p) n -> p c n", p=128)), writes=[r_wp], dma="t8")
                S.op("pool", lambda e: e.dma_start(out=w_pc, in_=wpc_d.rearrange("(c p) n -> p c n", p=128)), writes=[r_wp], dma="t9")

            def load_wo():
                S.op("pool", lambda e: e.dma_start(out=w_o, in_=wo_d.rearrange("(c p) n -> p c n", p=128)), writes=[r_wo], dma="t10")

            xt0_early = [False]

            def prep_bsp():
                ones_b = arena.alloc([64], BF16)
                S.op("dve", lambda e: e.memset(ones_b, 1.0), writes=[r_c])
                pb0, pr0 = next_bank()

                def mmrs(e):
                    last = None
                    for g8 in range(8):
                        gp, gl = divmod(g8, 2)
                        last = e.matmul(pb0[gl * 64:(gl + 1) * 64, gp * 128:(gp + 1) * 128],
                                        ones_b, wspT[:, g8, :], start=True, stop=True)
                    return last
                S.op("pe", mmrs, reads=[r_c, r_wsp], writes=[pr0])
                for gp in range(4):
                    S.op("dve", lambda e, gp=gp: e.scalar_tensor_tensor(
                        out=bsp[:, gp * 128:(gp + 1) * 128], in0=pb0[:, gp * 128:(gp + 1) * 128],
                        scalar=sgbT[:, gp:gp + 1], in1=bsp[:, gp * 128:(gp + 1) * 128], op0=ALU.mult, op1=ALU.add),
                        reads=[pr0, r_c], writes=[r_c])


            wg_v = wg_d.rearrange("(k p) d n -> p k d n", p=128)

            def load_xt(t):
                S.op("pool", lambda e, t=t: e.dma_start(out=xt[t % 2], in_=xT_v[:, :, t * TT:(t + 1) * TT]),
                     writes=[xtres[t % 2]], dma="xt%d" % (t % 2))

            def load_wg(gi):
                dmc = gi % KC
                S.op("pool", lambda e, gi=gi, dmc=dmc: e.dma_start(
                    out=wgs[gi % 3].rearrange("p k b n -> p k (b n)"), in_=wg_v[:, :, dmc, :]),
                    writes=[wgres[gi % 3]], dma="wg%d" % (gi % 3))

            load_xt(0)

            z_i = [0]
            g_i = [0]

            def stage_C1(t):
                xs, xr = xt[t % 2], xtres[t % 2]
                for c in range(2):
                    pb, pr = next_bank()
                    S.op("pe", mm_acc(pb, [(w_bc[:, k, 1024 + c * 128:1024 + (c + 1) * 128], xs[:, k, :]) for k in range(KC)]),
                         reads=[r_wbc, xr], writes=[pr])
                    S.op("act", lambda e, c=c, pb=pb: e.mul(qcT[:, c, :], pb, 0.125), reads=[pr], writes=[qcres[c]])
                for h in range(4):
                    c, hh = divmod(h, 2)
                    nr = slice(hh * 64, hh * 64 + 64)
                    for mc in range(2):
                        pb, pr = next_bank()
                        S.op("pe", mm_acc(pb, [(kcT[:, h, mc * 128:(mc + 1) * 128], qcT[:, c, :])]),
                             reads=[r_kv, qcres[c]], writes=[pr])
                        k_ = h * 2 + mc
                        S.op("act", lambda e, k_=k_, pb=pb: e.activation(out=ptC[k_], in_=pb, func=AF.Exp),
                             reads=[pr], writes=[ptCres[k_]])

            def stage_C2(t):
                for h in range(4):
                    c, hh = divmod(h, 2)
                    nr = slice(hh * 64, hh * 64 + 64)
                    dr = slice((1 - hh) * 64, (1 - hh) * 64 + 64)
                    sl2 = slice(0, 2) if hh == 0 else slice(1, 3)
                    ob, orr = next_bank()
                    S.op("pe", mm_acc(ob, [(vca[:, mc, c, sl2, :].rearrange("p a b -> p (a b)"), ptC[h * 2 + mc]) for mc in range(2)]),
                         reads=[r_kv, ptCres[h * 2], ptCres[h * 2 + 1]], writes=[orr])
                    S.op("dve", lambda e, nr=nr, dr=dr, ob=ob: e.reciprocal(out=recC[nr, :], in_=ob[dr, :]),
                         reads=[orr], writes=[r_recC])
                    S.op("dve", lambda e, nr=nr, c=c, ob=ob: e.tensor_tensor(out=ocT[nr, c, :], in0=ob[nr, :], in1=recC[nr, :], op=ALU.mult),
                         reads=[orr, r_recC], writes=[ocres[c]])

            def stage_B1(t):
                xs, xr = xt[t % 2], xtres[t % 2]
                for fc in range(4):
                    pb, pr = next_bank()
                    S.op("pe", mm_acc(pb, [(w_bc[:, k, fc * 128:(fc + 1) * 128], xs[:, k, :]) for k in range(KC)]),
                         reads=[r_wbc, xr], writes=[pr])
                    S.op("act", lambda e, fc=fc, pb=pb: e.activation(out=uT[:, fc, :], in_=pb, func=AF.Gelu_apprx_tanh),
                         reads=[pr], writes=[ures[fc]])
                for sub in range(4):
                    ts_ = slice(sub * 128, (sub + 1) * 128)
                    pb, pr = next_bank()
                    S.op("pe", mm_acc(pb, [(xs[:, k, ts_], w_bc[:, k, 512:1024]) for k in range(KC)]),
                         reads=[r_wbc, xr], writes=[pr])
                    S.op("act", lambda e, sub=sub, pb=pb: e.activation(out=vgf[sub], in_=pb, func=AF.Gelu_apprx_tanh),
                         reads=[pr], writes=[vgfres[sub]])
                for sub in range(4):
                    ln_norm(vgf[sub], vgfres[sub], 512)
                for sub in range(4):
                    ln_affine(vgf[sub], vgfres[sub], sgg, None, r_c, vgn[sub], vgnres[sub], "pool")

            def stage_B2(t):
                for sub in range(4):
                    ts_ = slice(sub * 128, (sub + 1) * 128)
                    pb2, pr2 = next_bank()

                    def mmsv(e, pb2=pb2, sub=sub):
                        last = None
                        for g8 in range(8):
                            gp, gl = divmod(g8, 2)
                            last = e.matmul(pb2[gl * 64:(gl + 1) * 64, gp * 128:(gp + 1) * 128],
                                            vgn[sub][:, g8 * 64:(g8 + 1) * 64], wspT[:, g8, :], start=True, stop=True)
                        return last
                    S.op("pe", mmsv, reads=[vgnres[sub], r_wsp], writes=[pr2])
                    S.op("dve", lambda e, pb2=pb2: e.tensor_tensor(out=tmpB, in0=pb2, in1=bsp, op=ALU.add),
                         reads=[pr2, r_c], writes=[r_tmpB])
                    S.op("dve", lambda e, ts_=ts_: e.tensor_tensor(
                        out=obT[:, :, ts_], in0=tmpB.rearrange("p (a b) -> p a b", a=4), in1=uT[:, :, ts_], op=ALU.mult),
                        reads=[r_tmpB] + ures, writes=[r_obT])

            def stage_G(t):
                xs, xr = xt[t % 2], xtres[t % 2]
                tok = slice(t * TT, (t + 1) * TT)
                gsets = {}

                def gates(dmc):
                    gi = g_i[0]
                    g_i[0] += 1
                    ws = wgs[gi % 3]
                    gts = []
                    for b in range(3):
                        pb, pr = next_bank()
                        S.op("pe", mm_acc(pb, [(ws[:, k, b, :], xs[:, k, :]) for k in range(KC)]),
                             reads=[wgres[gi % 3], xr], writes=[pr])
                        k_ = (gi % 2) * 3 + b
                        S.op("act", lambda e, k_=k_, pb=pb, b=b, dmc=dmc: e.activation(
                            out=gate[k_], in_=pb, func=AF.Sigmoid, bias=bg[:, b * 8 + dmc:b * 8 + dmc + 1], scale=1.0),
                            reads=[pr, r_c], writes=[gres[k_]])
                        gts.append(k_)
                    if gi + 3 < NT * KC:
                        load_wg(gi + 3)
                    gsets[dmc] = gts

                def proj(dmc):
                    fs = slice(dmc * 128, (dmc + 1) * 128)
                    gts = gsets[dmc]
                    srcs = [(w_pa, 2, lambda c: oaT[:, c, tok], oares),
                            (w_pb, 4, lambda c: obT[:, c, :], [r_obT]),
                            (w_pc, 2, lambda c: ocT[:, c, :], ocres)]
                    for b in range(3):
                        wp, ncn, rhs_f, rr = srcs[b]
                        pb, pr = next_bank()
                        S.op("pe", mm_acc(pb, [(wp[:, c, fs], rhs_f(c)) for c in range(ncn)]),
                             reads=[r_wp] + list(rr), writes=[pr])
                        S.op("dve", lambda e, b=b, pb=pb, k_=gts[b]: e.tensor_tensor(out=mt[b], in0=pb, in1=gate[k_], op=ALU.mult),
                             reads=[pr, gres[gts[b]]], writes=[mres[b]])
                    S.op("dve", lambda e: e.tensor_tensor(out=mt[0], in0=mt[0], in1=mt[1], op=ALU.add),
                         reads=[mres[1]], writes=[mres[0]])
                    S.op("dve", lambda e, dmc=dmc: e.tensor_tensor(out=mergedT[:, dmc, :], in0=mt[0], in1=mt[2], op=ALU.add),
                         reads=[mres[0], mres[2]], writes=[mgres[dmc]])

                for dmc in range(KC + 1):
                    if dmc < KC:
                        gates(dmc)
                    if dmc >= 1:
                        proj(dmc - 1)

            def load_x(t):
                for sub in range(4):
                    row0 = t * TT + sub * 128
                    S.op("sp", lambda e, sub=sub, row0=row0: e.dma_start(out=zt[sub], in_=x_d[row0:row0 + 128, :]),
                         writes=[zres[sub]], dma="xz%d" % sub)

            def stage_Y(t):
                for sub in range(4):
                    ts_ = slice(sub * 128, (sub + 1) * 128)
                    z_ = sub
                    row0 = t * TT + sub * 128
                    for half in range(2):
                        hs_ = slice(half * 512, (half + 1) * 512)
                        pb, pr = next_bank()
                        S.op("pe", mm_acc(pb, [(mergedT[:, k, ts_], w_o[:, k, hs_]) for k in range(KC)]),
                             reads=mgres + [r_wo], writes=[pr])
                        S.op("dve", lambda e, z_=z_, hs_=hs_, pb=pb: e.scalar_tensor_tensor(
                            out=zt[z_][:, hs_], in0=zt[z_][:, hs_], scalar=ALPHA, in1=pb, op0=ALU.mult, op1=ALU.add),
                            reads=[pr], writes=[zres[z_]])
                    ln_norm(zt[z_], zres[z_], DM)
                for sub in range(4):
                    z_ = sub
                    row0 = t * TT + sub * 128
                    ln_affine(zt[z_], zres[z_], ln1g, ln1b, r_c, zt[z_], zres[z_], "dve" if t == NT - 1 else "pool")
                    S.op("pool", lambda e, z_=z_, row0=row0: e.dma_start(out=h1_d[row0:row0 + 128, :], in_=zt[z_]),
                         reads=[zres[z_]], writes=[h1res[t][sub]], dma="h1w%d" % z_)
                if t + 1 < NT:
                    load_x(t + 1)

            load_x(0)
            load_wg(0)
            load_wp()
            load_wg(1)
            load_wg(2)
            load_xt(1)
            load_wo()
            stage_C1(0)
            stage_B1(0)
            prep_bsp()
            for t in range(NT):
                stage_C2(t)
                stage_B2(t)
                stage_G(t)
                if t == NT - 1:
                    load_up_chunk(0, guards=oares + [r_kv, r_wsp, r_wbc])
                if t + 1 < NT:
                    if t + 2 < NT:
                        load_xt(t + 2)
                    stage_C1(t + 1)
                    stage_B1(t + 1)
                stage_Y(t)

        outres = Res("out")

        def phase_F():
            w_dn = arena.alloc([NJ, DM], BF16)
            wdn_v = wdn_d.rearrange("(j p) n -> p j n", p=128)
            dnres = [Res() for _ in range(2)]
            for ci in range(1, len(jch)):
                load_up_chunk(ci)
            for q in range(2):
                S.op("pool", lambda e, q=q: e.dma_start(out=w_dn[:, q * 11:(q + 1) * 11, :], in_=wdn_v[:, q * 11:(q + 1) * 11, :]),
                     writes=[dnres[q]], dma="f_dn%d" % q)

            cw = arena.alloc([NJ, 3], F32)
            cb = arena.alloc([NJ], F32)
            ln2g = arena.alloc([DM], F32)
            ln2b = arena.alloc([DM], F32)
            halo = arena.alloc([NJ, 2], F32)
            r_c = Res("cF")
            r_halo = [Res() for _ in range(NJ)]
            S.op("sp", lambda e: e.dma_start(out=cw, in_=cw_d[:, :, :]), writes=[r_c], dma="f0")
            S.op("sp", lambda e: e.dma_start(out=cb, in_=cb_d[:, :]), writes=[r_c], dma="f1")
            S.op("sp", lambda e: e.dma_start(out=ln2g, in_=ln2g_d[0:1, :].broadcast_to([128, DM])), writes=[r_c], dma="f2")
            S.op("sp", lambda e: e.dma_start(out=ln2b, in_=ln2b_d[0:1, :].broadcast_to([128, DM])), writes=[r_c], dma="f3")
            S.op("dve", lambda e: e.memset(halo, 0.0), writes=r_halo)
            h1a = [arena.alloc([DM], F32) for _ in range(2)]
            h1ares = [Res(), Res()]
            h1b = [arena.alloc([DM], F32) for _ in range(2)]
            h1bres = [Res(), Res()]
            h1T = [arena.alloc([KC, TT], BF16) for _ in range(2)]
            h1Tres = [[Res() for _ in range(4)] for _ in range(2)]
            aext = [arena.alloc([TT + 2], F32) for _ in range(2)]
            ares = [Res(), Res()]
            cc = [arena.alloc([TT], F32) for _ in range(2)]
            ccres = [Res(), Res()]
            gT = arena.alloc([NJ, TT], BF16)
            gres = [Res() for _ in range(NJ)]
            a_i = [0]
            z_i = [0]
            tp_i = [0]

            def stage_tr(t):
                hT = h1T[t % 2]
                for sub in range(4):
                    s_ = (t * 4 + sub) % 2
                    row0 = t * TT + sub * 128
                    S.op("sp", lambda e, s_=s_, row0=row0: e.dma_start(out=h1a[s_], in_=h1_d[row0:row0 + 128, :]),
                         reads=[h1res[t][sub]], writes=[h1ares[s_]], dma="h1a%d" % s_)
                    b0 = (tp_i[0] % 2) * 2
                    tp_i[0] += 1

                    def tr(e, s_=s_, b0=b0):
                        last = None
                        for k in range(KC):
                            last = e.transpose(bank[b0 + k // 4][:, (k % 4) * 128:(k % 4 + 1) * 128],
                                               h1a[s_][:, k * 128:(k + 1) * 128], ident_f)
                        return last
                    S.op("pe", tr, reads=[h1ares[s_], r_const], writes=[bres[b0], bres[b0 + 1]])
                    for hb in range(2):
                        S.op("act", lambda e, sub=sub, b0=b0, hb=hb, hT=hT: e.copy(
                            out=hT[:, hb * 4:(hb + 1) * 4, sub * 128:(sub + 1) * 128],
                            in_=bank[b0 + hb].rearrange("p (b c) -> p b c", c=128)),
                            reads=[bres[b0 + hb]], writes=[h1Tres[t % 2][sub]])

            def stage_up(t):
                hT = h1T[t % 2]
                hTr = h1Tres[t % 2]
                for j in range(NJ):
                    a_ = a_i[0] % 2
                    a_i[0] += 1
                    pa, pra = next_bank(4, 4)
                    ca = j * 128
                    S.op("pe", mm_acc(pa, [(w_up_a[jchunk(j)][:, k, (j - jch[jchunk(j)][0]) * 128:(j - jch[jchunk(j)][0] + 1) * 128], hT[:, k, :]) for k in range(KC)]),
                         reads=[upres_a[jchunk(j)]] + hTr, writes=[pra])
                    pb, prb = next_bank(4, 4)
                    cbb = D_FF + j * 128
                    S.op("pe", mm_acc(pb, [(w_up_b[jchunk(j)][:, k, (j - jch[jchunk(j)][0]) * 128:(j - jch[jchunk(j)][0] + 1) * 128], hT[:, k, :]) for k in range(KC)]),
                         reads=[upres_b[jchunk(j)]] + hTr, writes=[prb])
                    S.op("act", lambda e, a_=a_, pa=pa: e.copy(out=aext[a_][:, 2:TT + 2], in_=pa),
                         reads=[pra], writes=[ares[a_]])
                    S.op("act", lambda e, a_=a_, j=j, pa=pa: e.activation(
                        out=cc[a_], in_=pa, func=AF.Identity, bias=cb[:, j:j + 1], scale=cw[:, j, 2:3]),
                        reads=[pra, r_c], writes=[ccres[a_]])
                    S.op("act", lambda e, a_=a_, j=j: e.copy(out=aext[a_][:, 0:2], in_=halo[:, j, :]),
                         reads=[r_halo[j]], writes=[ares[a_]])
                    S.op("act", lambda e, a_=a_, j=j: e.copy(out=halo[:, j, :], in_=aext[a_][:, TT:TT + 2]),
                         reads=[ares[a_]], writes=[r_halo[j]])
                    S.op("dve", lambda e, a_=a_, j=j: e.scalar_tensor_tensor(
                        out=cc[a_], in0=aext[a_][:, 1:TT + 1], scalar=cw[:, j, 1:2], in1=cc[a_], op0=ALU.mult, op1=ALU.add),
                        reads=[ares[a_]], writes=[ccres[a_]])
                    S.op("dve", lambda e, a_=a_, j=j: e.scalar_tensor_tensor(
                        out=cc[a_], in0=aext[a_][:, 0:TT], scalar=cw[:, j, 0:1], in1=cc[a_], op0=ALU.mult, op1=ALU.add),
                        reads=[ares[a_]], writes=[ccres[a_]])
                    S.op("act", lambda e, a_=a_: e.activation(out=cc[a_], in_=cc[a_], func=AF.Gelu_apprx_tanh),
                         reads=[ccres[a_]], writes=[ccres[a_]])
                    S.op("dve", lambda e, a_=a_, j=j, pb=pb: e.tensor_tensor(out=gT[:, j, :], in0=pb, in1=cc[a_], op=ALU.mult),
                         reads=[prb, ccres[a_]], writes=[gres[j]])

            def stage_down(t):
                for sub in range(4):
                    ts_ = slice(sub * 128, (sub + 1) * 128)
                    z_ = z_i[0] % 2
                    z_i[0] += 1
                    row0 = t * TT + sub * 128
                    S.op("sp", lambda e, z_=z_, row0=row0: e.dma_start(out=h1b[z_], in_=h1_d[row0:row0 + 128, :]),
                         reads=[h1res[t][sub]], writes=[h1bres[z_]], dma="h1b%d" % z_)
                    for half in range(2):
                        hs_ = slice(half * 512, (half + 1) * 512)
                        pb, pr = next_bank(4, 4)
                        S.op("pe", mm_acc(pb, [(gT[:, j, ts_], w_dn[:, j, hs_]) for j in range(NJ)]),
                             reads=gres + dnres, writes=[pr])
                        S.op("dve", lambda e, z_=z_, hs_=hs_, pb=pb: e.scalar_tensor_tensor(
                            out=h1b[z_][:, hs_], in0=h1b[z_][:, hs_], scalar=ALPHA, in1=pb, op0=ALU.mult, op1=ALU.add),
                            reads=[pr], writes=[h1bres[z_]])
                    ln_norm(h1b[z_], h1bres[z_], DM)
                    ln_affine(h1b[z_], h1bres[z_], ln2g, ln2b, r_c, h1b[z_], h1bres[z_], "dve" if t == NT - 1 else "pool")
                    S.op("pool", lambda e, z_=z_, row0=row0: e.dma_start(out=out_d[row0:row0 + 128, :], in_=h1b[z_]),
                         reads=[h1bres[z_]], writes=[outres], dma="ow%d" % z_)

            stage_tr(0)
            for t in range(NT):
                stage_up(t)
                if t + 1 < NT:
                    stage_tr(t + 1)
                stage_down(t)

        print('arena after A', arena.peak)
        S.barrier()
        arena.top = mark_T
        arena.peak = 0
        phase_T()
        print('arena T', arena.peak)
        if stop_after == "T":
            S.wait_all("sp", [r for rr in h1res for r in rr])
        else:
            S.barrier()
            arena.top = f_base
            phase_F()
            S.wait_all("sp", [outres])
        with nc.Block() as block:
            S.emit_all(block)
        print("SBUF arena peak units", arena.peak, "of", arena.n)
    return nc


def _t5_bucket(dist):
    max_exact = 16
    n = np.maximum(dist, 1).astype(np.float32)
    val = (np.log(n / np.float32(max_exact)) / np.float32(math.log(2048 / max_exact))
           * np.float32(32 - max_exact))
    large = max_exact + val.astype(np.int32)
    large = np.minimum(large, 31)
    return np.where(dist < max_exact, dist, large)


def _bias_tiles(rel_bias):
    kj = np.arange(128)[:, None]
    qc = np.arange(256)[None, :]
    steps = qc - kj
    valid = (steps >= 0) & (steps <= 128)
    out = np.empty((128, 12, 256), np.float32)
    for g, d in enumerate(DIL):
        bucket = _t5_bucket(np.maximum(steps, 0) * d)
        for h in range(4):
            col = g * 4 + h
            out[:, col, :] = np.where(valid, rel_bias[bucket, col], np.float32(-1e30))
    return out


def _prep_shared(inp):
    f = lambda a: np.ascontiguousarray(a, dtype=np.float32)
    sh = {}
    sh["biasA"] = _bias_tiles(f(inp["rel_bias"]))
    sh["ident"] = np.eye(128, dtype=np.float32)
    sh["w_in"] = f(inp["w_in"][0])
    wq = np.empty((DM, 6, 384), np.float32)
    for it in range(6):
        p, g = divmod(it, 3)
        for i, base in enumerate((0, 768, 1536)):
            c0 = base + g * 256 + p * 128
            wq[:, it, i * 128:(i + 1) * 128] = inp["w_in"][0][:, c0:c0 + 128]
    sh["w_qkv_r"] = wq
    sh["w_mem_kv"] = f(inp["w_mem_kv"][0])
    sh["sg_ln_g"] = f(inp["sg_ln_g"][0][None, :])
    sh["sg_ln_bT"] = f(inp["sg_ln_b"][0].reshape(4, 128).T)
    sh["wspT"] = f(np.transpose(inp["w_spatial"][0], (2, 0, 1)))
    sh["trilT"] = f(np.triu(np.ones((128, 128), np.float32)))
    bsp = inp["b_spatial"][0]
    t = bsp.reshape(4, 2, 128)
    t = np.transpose(t, (1, 0, 2))
    sh["bspT"] = f(np.broadcast_to(t[:, None, :, :], (2, 64, 4, 128)).reshape(128, 4, 128))
    sh["w_proj_a"] = f(inp["w_proj_a"][0])
    sh["w_proj_b"] = f(inp["w_proj_b"][0])
    sh["w_proj_c"] = f(inp["w_proj_c"][0])
    sh["w_gate_r"] = f(np.transpose(inp["w_gate"][0].reshape(DM, 3, 8, 128), (0, 2, 1, 3)).reshape(DM, 8, 384))
    sh["b_gateT"] = f(inp["b_gate"][0].reshape(24, 128).T)
    sh["w_out"] = f(inp["w_out"][0])
    sh["ln1_g"] = f(inp["ln1_g"][0][None, :])
    sh["ln1_b"] = f(inp["ln1_b"][0][None, :])
    sh["w_ffn_up"] = f(inp["w_ffn_up"][0])
    sh["conv_wT"] = f(np.transpose(inp["conv_w"][0].T.reshape(NJ, 128, 3), (1, 0, 2)))
    sh["conv_bT"] = f(inp["conv_b"][0].reshape(NJ, 128).T)
    sh["w_ffn_down"] = f(inp["w_ffn_down"][0])
    sh["ln2_g"] = f(inp["ln2_g"][0][None, :])
    sh["ln2_b"] = f(inp["ln2_b"][0][None, :])
    return sh


def _in_maps(inp, cores):
    sh = _prep_shared(inp)
    maps = []
    for b in cores:
        m = dict(sh)
        xb = np.ascontiguousarray(inp["x"][b], dtype=np.float32)
        m["x"] = xb
        m["xT"] = np.ascontiguousarray(xb.T)
        m["memT"] = np.ascontiguousarray(np.asarray(inp["mem"][b], dtype=np.float32).T)
        maps.append(m)
    return maps


def kernel(**inputs):
    nc = build_program()
    maps = _in_maps(inputs, list(range(N_CORES)))
    res = run_bass_kernel_spmd(nc, maps, core_ids=list(range(N_CORES)))
    out = np.stack([np.asarray(r["out"], dtype=np.float32) for r in res.results], axis=0)
    return out
```
